# Optimizing a Trainium2 kernel written in Bass

```python
import math
import jax, jax.numpy as jnp
from jax import lax
import numpy as np

D_MODEL = 1024
BATCH = 8
SEQ = 2048
DEPTH = 4

HEAD_DIM = 64
N_A_LAYERS = DEPTH // 2
N_B_LAYERS = DEPTH - N_A_LAYERS
DIFF_HEADS = D_MODEL // (2 * HEAD_DIM)
DIL_GROUPS = ((128, 1), (512, 4), (2048, 16))
N_GROUPS = len(DIL_GROUPS)
DIL_HEADS = D_MODEL // HEAD_DIM
D_FF = 4 * D_MODEL
ROPE_THETA = 10000.0
BLOCK = 128
LAM_STD = 0.1
EPS = 1e-6

kernel_name = 'yoco_diffattn_dilated_hybrid'


def rms_norm(x, gain):
    xf = x.astype(jnp.float32)
    y = xf * lax.rsqrt(jnp.mean(xf * xf, axis=-1, keepdims=True) + EPS)
    return (y * gain.astype(jnp.float32)).astype(x.dtype)


def rope_tables(seq, dim):
    inv = 1.0 / (ROPE_THETA ** (jnp.arange(0, dim, 2, dtype=jnp.float32) / dim))
    ang = jnp.arange(seq, dtype=jnp.float32)[:, None] * inv[None, :]
    return jnp.cos(ang), jnp.sin(ang)


def apply_rope(x, cos, sin):
    d2 = x.shape[-1] // 2
    x1, x2 = x[..., :d2], x[..., d2:]
    c = cos[None, :, None, :].astype(x.dtype)
    s = sin[None, :, None, :].astype(x.dtype)
    return jnp.concatenate([x1 * c - x2 * s, x2 * c + x1 * s], axis=-1)


def squared_relu_mlp(x, gain, w_up, w_down):
    h = rms_norm(x, gain)
    return jnp.square(jax.nn.relu(h @ w_up)) @ w_down


def diff_attention(h, w_qkv, q_gain, k_gain, lam_q1, lam_k1, lam_q2, lam_k2, sub_gain, lam_init, cos, sin):
    B, S, _ = h.shape
    H, D = DIFF_HEADS, HEAD_DIM
    q, k, v = jnp.split(h @ w_qkv, 3, axis=-1)
    q = apply_rope(rms_norm(q.reshape(B, S, 2 * H, D), q_gain), cos, sin).reshape(B, S, H, 2, D)
    k = apply_rope(rms_norm(k.reshape(B, S, 2 * H, D), k_gain), cos, sin).reshape(B, S, H, 2, D)
    v = v.reshape(B, S, H, 2 * D)
    f32 = jnp.float32
    lam = (jnp.exp(jnp.sum(lam_q1.astype(f32) * lam_k1.astype(f32)))
           - jnp.exp(jnp.sum(lam_q2.astype(f32) * lam_k2.astype(f32))) + lam_init)
    scale = D ** -0.5
    outs = []
    for i in range(S // BLOCK):
        n_k = (i + 1) * BLOCK
        q_blk = q[:, i * BLOCK:n_k]
        s = jnp.einsum('bqhcd,bkhcd->bhcqk', q_blk, k[:, :n_k]).astype(f32) * scale
        qpos = i * BLOCK + jnp.arange(BLOCK)
        kpos = jnp.arange(n_k)
        s = jnp.where(kpos[None, :] <= qpos[:, None], s, -jnp.inf)
        p = jax.nn.softmax(s, axis=-1)
        a = p[:, :, 0] - lam * p[:, :, 1]
        outs.append(jnp.einsum('bhqk,bkhe->bqhe', a.astype(v.dtype), v[:, :n_k]))
    o = jnp.concatenate(outs, axis=1)
    o = rms_norm(o, sub_gain) * (1.0 - lam_init)
    return o.reshape(B, S, H * 2 * D)


def dilated_group_attention(q, k, v, window, dilation):
    B, S, H, D = q.shape
    r = dilation
    L = S // r
    span = window // dilation
    nb = -(-L // BLOCK)
    Lp = nb * BLOCK
    Z = B * r

    def to_sub(a):
        a = a.reshape(B, L, r, H, D).transpose(0, 2, 1, 3, 4).reshape(Z, L, H, D)
        return jnp.pad(a, ((0, 0), (0, Lp - L), (0, 0), (0, 0)))

    def with_prev(a):
        a = jnp.pad(a, ((0, 0), (BLOCK, 0), (0, 0), (0, 0))).reshape(Z, nb + 1, BLOCK, H, D)
        return jnp.concatenate([a[:, :-1], a[:, 1:]], axis=2)

    qb = to_sub(q).reshape(Z, nb, BLOCK, H, D)
    kb = with_prev(to_sub(k))
    vb = with_prev(to_sub(v))
    s = jnp.einsum('znqhd,znkhd->znhqk', qb, kb).astype(jnp.float32) * (D ** -0.5)
    blk = jnp.arange(nb)[:, None, None] * BLOCK
    qi = blk + jnp.arange(BLOCK)[None, :, None]
    kj = blk - BLOCK + jnp.arange(2 * BLOCK)[None, None, :]
    dist = qi - kj
    valid = (dist >= 0) & (dist <= span) & (kj >= 0)
    s = jnp.where(valid[None, :, None], s, -jnp.inf)
    m = jnp.max(s, axis=-1, keepdims=True)
    p = jnp.exp(s - m)
    den = jnp.sum(p, axis=-1)
    o = jnp.einsum('znhqk,znkhd->znqhd', p.astype(vb.dtype), vb).astype(jnp.float32)
    o = o / jnp.transpose(den, (0, 1, 3, 2))[..., None]
    lse = jnp.transpose(m[..., 0] + jnp.log(den), (0, 1, 3, 2))
    o = o.reshape(Z, Lp, H, D)[:, :L].reshape(B, r, L, H, D).transpose(0, 2, 1, 3, 4).reshape(B, S, H, D)
    lse = lse.reshape(Z, Lp, H)[:, :L].reshape(B, r, L, H).transpose(0, 2, 1, 3).reshape(B, S, H)
    return o, lse


def shared_kv(x, kv_norm, kv_w, kv_k_gain, cos, sin):
    B, S, _ = x.shape
    G, H, D = N_GROUPS, DIL_HEADS, HEAD_DIM
    k, v = jnp.split(rms_norm(x, kv_norm) @ kv_w, 2, axis=-1)
    k = rms_norm(k.reshape(B, S, G, H, D), kv_k_gain[:, None, :])
    k = apply_rope(k.reshape(B, S, G * H, D), cos, sin).reshape(B, S, G, H, D)
    return k, v.reshape(B, S, G, H, D)


def dilated_mixer(x, norm_gain, w_q, q_gain, w_o, k_sh, v_sh, cos, sin):
    B, S, _ = x.shape
    G, H, D = N_GROUPS, DIL_HEADS, HEAD_DIM
    q = rms_norm(rms_norm(x, norm_gain) @ w_q, jnp.ones((), x.dtype)).reshape(B, S, G, H, D) if False else (rms_norm(x, norm_gain) @ w_q).reshape(B, S, G, H, D)
    q = rms_norm(q, q_gain[:, None, :])
    q = apply_rope(q.reshape(B, S, G * H, D), cos, sin).reshape(B, S, G, H, D)
    outs, lses = [], []
    for g, (window, dilation) in enumerate(DIL_GROUPS):
        o_g, lse_g = dilated_group_attention(q[:, :, g], k_sh[:, :, g], v_sh[:, :, g], window, dilation)
        outs.append(o_g)
        lses.append(lse_g)
    wts = jax.nn.softmax(jnp.stack(lses, axis=0), axis=0)
    o = jnp.sum(wts[..., None] * jnp.stack(outs, axis=0), axis=0)
    return o.astype(x.dtype).reshape(B, S, H * D) @ w_o


def setup_inputs(seed: int = 0) -> dict:
    key = jax.random.key(seed)
    ks = jax.random.split(key, 21)
    f32 = jnp.float32
    nA, nB, G, D = N_A_LAYERS, N_B_LAYERS, N_GROUPS, HEAD_DIM

    def w(k, shape, fan_in):
        return jax.random.normal(k, shape, f32) * (fan_in ** -0.5)

    def gain(k, shape):
        return 1.0 + 0.02 * jax.random.normal(k, shape, f32)

    return {
        'x': jax.random.normal(ks[0], (BATCH, SEQ, D_MODEL), f32),
        'a_norm': gain(ks[1], (nA, D_MODEL)),
        'a_w_qkv': w(ks[2], (nA, D_MODEL, 3 * D_MODEL), D_MODEL),
        'a_q_gain': gain(ks[3], (nA, D)),
        'a_k_gain': gain(ks[4], (nA, D)),
        'a_lam_q1': LAM_STD * jax.random.normal(ks[5], (nA, D), f32),
        'a_lam_k1': LAM_STD * jax.random.normal(ks[6], (nA, D), f32),
        'a_lam_q2': LAM_STD * jax.random.normal(ks[7], (nA, D), f32),
        'a_lam_k2': LAM_STD * jax.random.normal(ks[8], (nA, D), f32),
        'a_sub_gain': gain(ks[9], (nA, 2 * D)),
        'a_w_o': w(ks[10], (nA, D_MODEL, D_MODEL), D_MODEL),
        'kv_norm': gain(ks[11], (D_MODEL,)),
        'kv_w': w(ks[12], (D_MODEL, 2 * G * DIL_HEADS * D), D_MODEL),
        'kv_k_gain': gain(ks[13], (G, D)),
        'b_norm': gain(ks[14], (nB, D_MODEL)),
        'b_w_q': w(ks[15], (nB, D_MODEL, G * DIL_HEADS * D), D_MODEL),
        'b_q_gain': gain(ks[16], (nB, G, D)),
        'b_w_o': w(ks[17], (nB, DIL_HEADS * D, D_MODEL), DIL_HEADS * D),
        'm_norm': gain(ks[18], (DEPTH, D_MODEL)),
        'm_w_up': w(ks[19], (DEPTH, D_MODEL, D_FF), D_MODEL),
        'm_w_down': w(ks[20], (DEPTH, D_FF, D_MODEL), D_FF),
    }


def reference(x, a_norm, a_w_qkv, a_q_gain, a_k_gain, a_lam_q1, a_lam_k1, a_lam_q2, a_lam_k2,
              a_sub_gain, a_w_o, kv_norm, kv_w, kv_k_gain, b_norm, b_w_q, b_q_gain, b_w_o,
              m_norm, m_w_up, m_w_down):
    S = x.shape[1]
    cos, sin = rope_tables(S, HEAD_DIM)
    k_sh, v_sh = None, None
    for layer in range(DEPTH):
        if layer < N_A_LAYERS:
            lam_init = 0.8 - 0.6 * math.exp(-0.3 * layer)
            h = rms_norm(x, a_norm[layer])
            att = diff_attention(h, a_w_qkv[layer], a_q_gain[layer], a_k_gain[layer],
                                 a_lam_q1[layer], a_lam_k1[layer], a_lam_q2[layer], a_lam_k2[layer],
                                 a_sub_gain[layer], lam_init, cos, sin)
            x = x + att @ a_w_o[layer]
        else:
            if layer == N_A_LAYERS:
                k_sh, v_sh = shared_kv(x, kv_norm, kv_w, kv_k_gain, cos, sin)
            bl = layer - N_A_LAYERS
            x = x + dilated_mixer(x, b_norm[bl], b_w_q[bl], b_q_gain[bl], b_w_o[bl], k_sh, v_sh, cos, sin)
        x = x + squared_relu_mlp(x, m_norm[layer], m_w_up[layer], m_w_down[layer])
    return x
```

```python
import math
from contextlib import ExitStack
import numpy as np
import concourse.bass as bass
import concourse.mybir as mybir
from concourse.bass_utils import run_bass_kernel_spmd

F32 = mybir.dt.float32
BF16 = mybir.dt.bfloat16
AF = mybir.ActivationFunctionType
ALU = mybir.AluOpType
AX = mybir.AxisListType

S = 2048
D = 1024
NT = 16
NCH = 8
DFF = 4096
EPS = 1e-6
NSEM_DMA = 8
GROUPS = ((128, 1), (512, 4), (2048, 16))


class Op:
    __slots__ = ("eng", "fn", "deps", "dma", "idx", "needs_inc", "ord", "sem_i", "sem_k", "name")

    def __init__(self, eng, fn, dma, name):
        self.eng = eng
        self.fn = fn
        self.dma = dma
        self.deps = set()
        self.needs_inc = False
        self.ord = 0
        self.sem_i = 0
        self.sem_k = 0
        self.name = name


def _is_psum(k):
    return k == "psT" or (isinstance(k, tuple) and k[0] in ("psP", "psS", "psA"))


class Sched:
    ENGS = ("pe", "act", "dve", "pool", "sp")

    def __init__(self):
        self.ops = {e: [] for e in self.ENGS}
        self.lastw = {}
        self.readers = {}
        self.ndma = {e: 0 for e in self.ENGS}
        self.dma_by_sem = {}

    def add(self, eng, fn, r=(), w=(), after=(), dma=False, name=""):
        op = Op(eng, fn, dma, name)
        deps = op.deps
        pr = [k for k in r if _is_psum(k)]
        if pr:
            w = list(w) + [k for k in pr if k not in w]
        for k in r:
            lw = self.lastw.get(k)
            if lw is not None:
                deps.add(lw)
        for k in w:
            lw = self.lastw.get(k)
            if lw is not None:
                deps.add(lw)
            for rd in self.readers.get(k, ()):
                deps.add(rd)
        for a in after:
            if a is not None:
                deps.add(a)
        for k in r:
            self.readers.setdefault(k, []).append(op)
        for k in w:
            self.lastw[k] = op
            self.readers[k] = []
        if dma:
            n = self.ndma[eng]
            self.ndma[eng] = n + 1
            op.sem_i = n % NSEM_DMA
            op.sem_k = n // NSEM_DMA + 1
            prev = self.dma_by_sem.get((eng, op.sem_i))
            if prev is not None:
                deps.add(prev)
            self.dma_by_sem[(eng, op.sem_i)] = op
        op.idx = len(self.ops[eng])
        self.ops[eng].append(op)
        return op

    def barrier(self):
        lasts = []
        for e in self.ENGS:
            if self.ops[e]:
                lasts.append(self.ops[e][-1])
        dmas = list(self.dma_by_sem.values())
        for e in self.ENGS:
            self.add(e, None, after=lasts + dmas, name="barrier")

    def finalize(self):
        for e in self.ENGS:
            for op in self.ops[e]:
                for d in op.deps:
                    if d.dma:
                        continue
                    if d.eng == "pe" and e == "pe":
                        continue
                    d.needs_inc = True
        for e in self.ENGS:
            c = 0
            for op in self.ops[e]:
                if op.needs_inc and not op.dma:
                    c += 1
                    op.ord = c

    def emit(self, e, engobj, sems, dsems):
        seen = {x: 0 for x in self.ENGS}
        seen_dma = {}
        for op in self.ops[e]:
            for d in sorted(op.deps, key=lambda o: (o.eng, o.idx)):
                if d.dma:
                    key = (d.eng, d.sem_i)
                    if seen_dma.get(key, 0) >= d.sem_k:
                        continue
                    engobj.wait_ge(dsems[d.eng][d.sem_i], 16 * d.sem_k)
                    seen_dma[key] = d.sem_k
                else:
                    if d.eng == "pe" and e == "pe":
                        continue
                    if seen[d.eng] >= d.ord:
                        continue
                    engobj.wait_ge(sems[d.eng], d.ord)
                    seen[d.eng] = d.ord
            if op.fn is None:
                if op.needs_inc:
                    engobj.sem_inc(sems[e], 1)
                continue
            ins = op.fn(engobj)
            if op.dma:
                ins.then_inc(dsems[e][op.sem_i], 16)
            elif op.needs_inc:
                ins.then_inc(sems[e], 1)


def _consts():
    ident = np.eye(128, dtype=np.float32)
    k = np.arange(128)[:, None]
    q = np.arange(128)[None, :]
    le = (k <= q).astype(np.float32)
    ge = (k >= q).astype(np.float32)
    z = np.zeros((128, 128), np.float32)
    masks = np.stack([np.concatenate([ge, le, ge, le], 1),
                      np.concatenate([z, le, ge, le], 1),
                      np.concatenate([z, le, z, le], 1)], 1)
    inv = 1.0 / (10000.0 ** (np.arange(0, 64, 2, dtype=np.float32) / np.float32(64)))
    t = np.arange(S, dtype=np.float32)
    ang = (t[:, None] * inv[None, :].astype(np.float32)).astype(np.float32)
    c = np.cos(ang).astype(np.float32)
    s = np.sin(ang).astype(np.float32)
    cs = np.concatenate([c, c], 1).reshape(NT, 128, 64).transpose(1, 0, 2)
    ss = np.concatenate([-s, s], 1).reshape(NT, 128, 64).transpose(1, 0, 2)
    sw = np.zeros((128, 128), np.float32)
    for m_ in range(128):
        sw[(m_ + 64) % 128, m_] = 1.0
    return dict(c_sw=sw, c_ident=ident, c_masks=np.ascontiguousarray(masks),
                c_negi=(-30000.0 * ident).astype(np.float32), c_nmasks=np.ascontiguousarray(1.0 - masks),
                c_cs=np.ascontiguousarray(cs), c_ss=np.ascontiguousarray(ss))


class StopBuild(Exception):
    pass


def build(layers=(0, 1, 2, 3), stop=None):
    nc = bass.Bass("TRN2", target_bir_lowering=False)

    def ck(name):
        if stop == name:
            raise StopBuild()
    sch = Sched()
    es = ExitStack()

    def din(name, shape):
        return nc.dram_tensor(name, list(shape), F32, kind="ExternalInput").ap()

    x_in = din("x", [S, D])
    a_norm = din("a_norm", [2, D])
    a_w_qkv = din("a_w_qkv", [2, D, 3 * D])
    a_q_gain = din("a_q_gain", [2, 64])
    a_k_gain = din("a_k_gain", [2, 64])
    a_lam = [din(n, [2, 64]) for n in ("a_lam_q1", "a_lam_k1", "a_lam_q2", "a_lam_k2")]
    a_sub_gain = din("a_sub_gain", [2, 128])
    a_w_o = din("a_w_o", [2, D, D])
    kv_norm = din("kv_norm", [D])
    kv_w = din("kv_w", [D, 6 * D])
    kv_k_gain = din("kv_k_gain", [3, 64])
    b_norm = din("b_norm", [2, D])
    b_w_q = din("b_w_q", [2, D, 3 * D])
    b_q_gain = din("b_q_gain", [2, 3, 64])
    b_w_o = din("b_w_o", [2, D, D])
    m_norm = din("m_norm", [4, D])
    m_w_up = din("m_w_up", [4, D, DFF])
    m_w_down = din("m_w_down", [4, DFF, D])
    c_ident = din("c_ident", [128, 128])
    c_sw = din("c_sw", [128, 128])
    c_negi = din("c_negi", [128, 128])
    c_nmasks = din("c_nmasks", [128, 3, 512])
    c_masks = din("c_masks", [128, 3, 512])
    c_cs = din("c_cs", [128, NT, 64])
    c_ss = din("c_ss", [128, NT, 64])
    y_out = nc.dram_tensor("y", [S, D], F32, kind="ExternalOutput").ap()
    kscr = nc.dram_tensor("kscr", [3, 8, 128, S], BF16, kind="Internal").ap()
    vscr = nc.dram_tensor("vscr", [3, 2, 128, NT, 512], BF16, kind="Internal").ap()

    def sb(name, shape, dt):
        return es.enter_context(nc.sbuf_tensor(name, list(shape), dt))

    def pst(name, shape, dt):
        return es.enter_context(nc.psum_tensor(name, list(shape), dt))

    xres = sb("xres", [128, NT, D], F32)
    hT = sb("hT", [128, NCH, S], BF16)
    ident = sb("ident", [128, 128], BF16)
    masks = sb("masks", [128, 3, 512], BF16)
    negi = sb("negi", [128, 128], BF16)
    cs_t = sb("cs_t", [128, NT, 64], F32)
    ss_t = sb("ss_t", [128, NT, 64], F32)
    gbc = sb("gbc", [128, D], F32)
    qkg = sb("qkg", [128, 512], F32)
    sgb = sb("sgb", [128, 128], F32)
    lamt = sb("lamt", [128, 4, 64], F32)
    lamv = sb("lamv", [128, 8], F32)
    swf = sb("swf", [128, 128], F32)
    gst = sb("gst", [128, 192], F32)
    hn = [sb(f"hn{i}", [128, D], BF16) for i in range(2)]
    ssq = [sb(f"ssq{i}", [128, 8], F32) for i in range(4)]
    REG = 88576
    regS = sb("regS", [128, REG], mybir.dt.uint8)

    class Carver:
        def __init__(self):
            self.off = 0

        def take(self, shape, dt):
            n = 1
            for s_ in shape[1:]:
                n *= s_
            nbytes = n * (4 if dt == F32 else 2)
            assert self.off + nbytes <= REG, (self.off, nbytes)
            v = regS[:, self.off:self.off + nbytes].bitcast(dt)
            self.off += nbytes
            if len(shape) == 3:
                v = v.rearrange("p (a b) -> p a b", b=shape[2])
            elif len(shape) == 4:
                v = v.rearrange("p (a b c) -> p a b c", b=shape[2], c=shape[3])
            return v

    psP = [pst(f"psP{i}", [128, 512], F32) for i in range(2)]
    psT = pst("psT", [128, 1024], BF16)
    psS = [pst(f"psS{i}", [128, 512], F32) for i in range(2)]
    psA = [pst(f"psA{i}", [128, 512], F32) for i in range(3)]

    psTb = psA[2][:].bitcast(BF16)
    TB = (psT, psTb)
    TK = ("psT", ("psA", 2))
    psTc = psP[1][:].bitcast(BF16)
    TB2 = (psT, psTc)
    SB3 = ((psS[0], ("psS", 0)), (psS[1], ("psS", 1)), (psP[1], ("psP", 1)))
    TK2 = ("psT", ("psP", 1))

    sems = {}
    dsems = {}
    for e in Sched.ENGS:
        sems[e] = es.enter_context(nc.semaphore(f"s_{e}"))
    for e in ("sp", "pool"):
        dsems[e] = [es.enter_context(nc.semaphore(f"d_{e}{i}")) for i in range(NSEM_DMA)]

    A = sch.add

    def C(name, *args, **kw):
        return lambda e: getattr(e, name)(*args, **kw)
    cnt = [0]

    def uid():
        cnt[0] += 1
        return cnt[0]

    def dma_sp(out, in_, r=(), w=(), after=()):
        return A("sp", C("dma_start", out=out, in_=in_), r=r, w=w, after=after, dma=True)

    def dma_pool(out, in_, r=(), w=(), after=()):
        return A("pool", C("dma_start", out=out, in_=in_), r=r, w=w, after=after, dma=True)

    dma_pool(ident[:], c_ident, w=["ident"])
    dma_pool(masks[:], c_nmasks, w=["masks"])
    dma_pool(negi[:], c_negi, w=["negi"])
    dma_sp(cs_t[:], c_cs, w=["cs"])
    dma_sp(swf[:], c_sw, w=["swf"])
    dma_sp(ss_t[:], c_ss, w=["ss"])
    xv = x_in.rearrange("(t p) d -> p t d", p=128)
    for g4 in range(4):
        dma_sp(xres[:, 4 * g4:4 * g4 + 4, :], xv[:, 4 * g4:4 * g4 + 4, :], w=[("x", t) for t in range(4 * g4, 4 * g4 + 4)])

    def rsqrt(out_ap, in_ap, c, keys):
        A("act", C("activation", out=out_ap, in_=in_ap, func=AF.Ln, bias=float(c)), r=keys, w=keys)
        A("act", C("activation", out=out_ap, in_=out_ap, func=AF.Exp, scale=-0.5), r=keys, w=keys)

    def load_bcast(dst_ap, src_row_ap, n, key, scale=None):
        dma_sp(dst_ap, src_row_ap.partition_broadcast(128), w=[key])
        if scale is not None:
            A("dve", C("tensor_scalar", out=dst_ap, in0=dst_ap, scalar1=float(scale), scalar2=None, op0=ALU.mult),
              r=[key], w=[key])

    def rmsnorm_to_hT(gain_row_ap):
        load_bcast(gbc[:], gain_row_ap, D, "gbc", scale=math.sqrt(D))
        for t in range(NT):
            sq = ssq[t % 4]
            hb = hn[t % 2]
            kx = ("x", t)
            A("act", C("activation", out=hb[:], in_=xres[:, t, :], func=AF.Square, accum_out=sq[:, 0:1]),
              r=[kx], w=[("ssq", t % 4), ("hn", t % 2)])
            rsqrt(sq[:, 1:2], sq[:, 0:1], D * EPS, [("ssq", t % 4)])
            A("dve", C("scalar_tensor_tensor", out=hb[:], in0=xres[:, t, :], scalar=sq[:, 1:2], in1=gbc[:],
                                                                        op0=ALU.mult, op1=ALU.mult),
              r=[kx, ("ssq", t % 4), "gbc"], w=[("hn", t % 2)])
            tb, tk = TB2[t % 2], TK2[t % 2]
            for c in range(NCH):
                A("pe", C("transpose", tb[:, c * 128:(c + 1) * 128], hb[:, c * 128:(c + 1) * 128], ident[:]),
                  r=[("hn", t % 2), "ident"], w=[tk])
            src = tb[:, :].rearrange("p (c k) -> p c k", k=128)
            eng = "act" if t % 2 == 0 else "dve"
            if eng == "act":
                A("act", C("copy", out=hT[:, :, t * 128:(t + 1) * 128], in_=src), r=[tk], w=[("hT", t)])
            else:
                A("dve", C("tensor_copy", out=hT[:, :, t * 128:(t + 1) * 128], in_=src), r=[tk], w=[("hT", t)])

    def qk_norm_rope_g(ps_ap, ncomp, t, out_bf, keys_r, key_w, tmp):
        n = ncomp * 64
        sqb, ya, yb, rs = tmp["sq"], tmp["ya"], tmp["yb"], tmp["rs"]
        kk = tmp["key"]
        A("act", C("activation", out=sqb[:, 0:n], in_=ps_ap, func=AF.Square), r=keys_r, w=[(kk, "sq")])
        yield
        A("dve", C("tensor_reduce", out=rs[:, 0:ncomp], in_=sqb[:, 0:n].rearrange("p (c d) -> p c d", d=64),
                   axis=AX.X, op=ALU.add), r=[(kk, "sq")], w=[(kk, "rs")])
        yield
        rsqrt(rs[:, 8:8 + ncomp], rs[:, 0:ncomp], 64 * EPS, [(kk, "rs")])
        yield
        ps3 = ps_ap.rearrange("p (c d) -> p c d", d=64)
        ya3 = ya[:, 0:n].rearrange("p (c d) -> p c d", d=64)
        yb3 = yb[:, 0:n].rearrange("p (c d) -> p c d", d=64)
        rsb = rs[:, 8:8 + ncomp].unsqueeze(2).to_broadcast([128, ncomp, 64])
        A("dve", C("tensor_tensor", out=ya3, in0=ps3, in1=rsb, op=ALU.mult), r=keys_r + [(kk, "rs")], w=[(kk, "ya")])
        A("dve", C("tensor_tensor", out=ya[:, 0:n], in0=ya[:, 0:n], in1=qkg[:, 0:n], op=ALU.mult),
          r=[(kk, "ya"), "qkg"], w=[(kk, "ya")])
        csb = cs_t[:, t, :].unsqueeze(1).to_broadcast([128, ncomp, 64])
        s1b = ss_t[:, t, 0:32].unsqueeze(1).to_broadcast([128, ncomp, 32])
        s2b = ss_t[:, t, 32:64].unsqueeze(1).to_broadcast([128, ncomp, 32])
        A("dve", C("tensor_tensor", out=yb3[:, :, 0:32], in0=ya3[:, :, 32:64], in1=s1b, op=ALU.mult),
          r=[(kk, "ya"), "ss"], w=[(kk, "yb")])
        A("dve", C("tensor_tensor", out=yb3[:, :, 32:64], in0=ya3[:, :, 0:32], in1=s2b, op=ALU.mult),
          r=[(kk, "ya"), "ss"], w=[(kk, "yb")])
        A("dve", C("tensor_tensor", out=ya3, in0=ya3, in1=csb, op=ALU.mult), r=[(kk, "ya"), "cs"], w=[(kk, "ya")])
        A("dve", C("tensor_tensor", out=out_bf, in0=ya[:, 0:n], in1=yb[:, 0:n], op=ALU.add),
          r=[(kk, "ya"), (kk, "yb")], w=[key_w])
        yield

    def qk_norm_rope(ps_ap, ncomp, t, out_bf, keys_r, key_w, tmp):
        n = ncomp * 64
        sqb, ya, yb, rs = tmp["sq"], tmp["ya"], tmp["yb"], tmp["rs"]
        kk = tmp["key"]
        A("act", C("activation", out=sqb[:, 0:n], in_=ps_ap, func=AF.Square), r=keys_r, w=[(kk, "sq")])
        A("dve", C("tensor_reduce", out=rs[:, 0:ncomp], in_=sqb[:, 0:n].rearrange("p (c d) -> p c d", d=64),
                                           axis=AX.X, op=ALU.add), r=[(kk, "sq")], w=[(kk, "rs")])
        rsqrt(rs[:, 8:8 + ncomp], rs[:, 0:ncomp], 64 * EPS, [(kk, "rs")])
        ps3 = ps_ap.rearrange("p (c d) -> p c d", d=64)
        ya3 = ya[:, 0:n].rearrange("p (c d) -> p c d", d=64)
        yb3 = yb[:, 0:n].rearrange("p (c d) -> p c d", d=64)
        rsb = rs[:, 8:8 + ncomp].unsqueeze(2).to_broadcast([128, ncomp, 64])
        A("dve", C("tensor_tensor", out=ya3, in0=ps3, in1=rsb, op=ALU.mult), r=keys_r + [(kk, "rs")], w=[(kk, "ya")])
        A("dve", C("tensor_tensor", out=ya[:, 0:n], in0=ya[:, 0:n], in1=qkg[:, 0:n], op=ALU.mult),
          r=[(kk, "ya"), "qkg"], w=[(kk, "ya")])
        csb = cs_t[:, t, :].unsqueeze(1).to_broadcast([128, ncomp, 64])
        s1b = ss_t[:, t, 0:32].unsqueeze(1).to_broadcast([128, ncomp, 32])
        s2b = ss_t[:, t, 32:64].unsqueeze(1).to_broadcast([128, ncomp, 32])
        A("dve", C("tensor_tensor", out=yb3[:, :, 0:32], in0=ya3[:, :, 32:64], in1=s1b, op=ALU.mult),
          r=[(kk, "ya"), "ss"], w=[(kk, "yb")])
        A("dve", C("tensor_tensor", out=yb3[:, :, 32:64], in0=ya3[:, :, 0:32], in1=s2b, op=ALU.mult),
          r=[(kk, "ya"), "ss"], w=[(kk, "yb")])
        A("dve", C("tensor_tensor", out=ya3, in0=ya3, in1=csb, op=ALU.mult), r=[(kk, "ya"), "cs"], w=[(kk, "ya")])
        A("dve", C("tensor_tensor", out=out_bf, in0=ya[:, 0:n], in1=yb[:, 0:n], op=ALU.add),
          r=[(kk, "ya"), (kk, "yb")], w=[key_w])

    def out_proj_add(oT_ap, okeys, wo_ap, wkey):
        i = 0
        for t in range(NT):
            for h in range(2):
                pb = psA[i % 3]
                pk = ("psA", i % 3)
                i += 1
                A("pe", C("matmul", pb[:], oT_ap[:, t * 128:(t + 1) * 128], wo_ap[:, h * 512:(h + 1) * 512],
                                                            start=True, stop=True),
                  r=okeys(t) + [wkey], w=[pk])
                A("dve", C("tensor_tensor", out=xres[:, t, h * 512:(h + 1) * 512], in0=pb[:],
                                                                    in1=xres[:, t, h * 512:(h + 1) * 512], op=ALU.add),
                  r=[pk, ("x", t)], w=[("x", t)])

    def layer_a(l):
        cv = Carver()
        wsl = [cv.take([128, NCH, 384], BF16) for _ in range(2)]
        wos = [cv.take([128, D], BF16) for _ in range(2)]
        qT = [cv.take([128, S], BF16) for _ in range(2)]
        kT = [cv.take([128, S], BF16) for _ in range(2)]
        vh = [cv.take([128, NT, 130], BF16) for _ in range(2)]
        oT = [cv.take([128, S], BF16) for _ in range(2)]
        qkb = [cv.take([128, 256], BF16) for _ in range(2)]
        Et = [cv.take([128, 512], BF16) for _ in range(4)]
        tmp = dict(sq=cv.take([128, 512], F32), ya=cv.take([128, 512], F32), yb=cv.take([128, 512], F32),
                   rs=cv.take([128, 16], F32), key="tmpA")
        o1 = [cv.take([128, 128], F32) for _ in range(2)]
        o2 = [cv.take([128, 128], F32) for _ in range(2)]
        onb = [cv.take([128, 128], BF16) for _ in range(2)]
        sm = [cv.take([128, 8], F32) for _ in range(2)]
        lam_init = 0.8 - 0.6 * math.exp(-0.3 * l)

        rmsnorm_to_hT(a_norm[l])
        ck("norm")
        for j in range(2):
            dma_sp(qkg[:, j * 64:(j + 1) * 64], a_q_gain[l].partition_broadcast(128), r=["qkg"], w=[("qkgp", j)])
            dma_sp(qkg[:, 128 + j * 64:128 + (j + 1) * 64], a_k_gain[l].partition_broadcast(128), r=["qkg"], w=[("qkgp", 2 + j)])
        A("dve", C("tensor_scalar", out=qkg[:, 0:256], in0=qkg[:, 0:256], scalar1=8.0, scalar2=None, op0=ALU.mult),
          r=[("qkgp", j) for j in range(4)], w=["qkg"] + [("qkgp", j) for j in range(4)])
        ck("g1")
        load_bcast(sgb[:], a_sub_gain[l], 128, "sgb", scale=math.sqrt(128.0) * (1.0 - lam_init))
        ck("g2")
        for j in range(4):
            dma_sp(lamt[:, j, :], a_lam[j][l].partition_broadcast(128), r=["lamt"], w=[("lamtp", j)])
        A("dve", C("tensor_tensor", out=lamt[:, 0, :], in0=lamt[:, 0, :], in1=lamt[:, 1, :], op=ALU.mult),
          r=[("lamtp", j) for j in range(4)], w=["lamt"] + [("lamtp", j) for j in range(4)])
        A("dve", C("tensor_tensor", out=lamt[:, 2, :], in0=lamt[:, 2, :], in1=lamt[:, 3, :], op=ALU.mult), r=["lamt"], w=["lamt"])
        ck("g3")
        A("dve", C("tensor_reduce", out=lamv[:, 0:1], in_=lamt[:, 0, :], axis=AX.X, op=ALU.add), r=["lamt"], w=["lamv"])
        A("dve", C("tensor_reduce", out=lamv[:, 1:2], in_=lamt[:, 2, :], axis=AX.X, op=ALU.add), r=["lamt"], w=["lamv"])
        ck("g4")
        A("act", C("activation", out=lamv[:, 2:4], in_=lamv[:, 0:2], func=AF.Exp), r=["lamv"], w=["lamv"])
        ck("g5")
        A("dve", C("tensor_scalar", out=lamv[:, 5:6], in0=lamv[:, 2:3], scalar1=-1.0, scalar2=float(-lam_init),
                   op0=ALU.mult, op1=ALU.add), r=["lamv"], w=["lamv"])
        A("dve", C("tensor_tensor", out=lamv[:, 4:5], in0=lamv[:, 5:6], in1=lamv[:, 3:4], op=ALU.add), r=["lamv"], w=["lamv"])
        wq = a_w_qkv[l].rearrange("(c p) n -> p c n", p=128)
        ck("gains")
        for hd in range(8):
            b = hd % 2
            kws = [("wsl", b, j) for j in range(3)]
            for j in range(3):
                dma_pool(wsl[b][:, :, j * 128:(j + 1) * 128], wq[:, :, j * D + hd * 128: j * D + (hd + 1) * 128], w=[kws[j]])
            dma_pool(wos[b][:], a_w_o[l][hd * 128:(hd + 1) * 128, :], w=[("wos", b)])
            if hd == 0:
                for bb in range(2):
                    A("dve", C("memset", vh[bb][:, :, 128:130], 1.0), w=[("vh", bb, c4) for c4 in range(NT)])
            ck("p0")
            for t in range(NT):
                pb = psP[t % 2]
                pk = ("psP", t % 2)
                for c in range(NCH):
                    A("pe", C("matmul", pb[:, 0:384], hT[:, c, t * 128:(t + 1) * 128], wsl[b][:, c, :],
                                                                start=(c == 0), stop=(c == NCH - 1)),
                      r=[("hT", t)] + kws, w=[pk])
                ck("p1")
                A("act", C("copy", out=vh[b][:, t, 0:128], in_=pb[:, 256:384]), r=[pk], w=[("vh", b, t)])
                ck("p2")
                qb_ = qkb[t % 2]
                qk_norm_rope(pb[:, 0:256], 4, t, qb_[:], [pk], ("qkb", t % 2), tmp)
                ck("p3")
                j4 = t % 4
                A("pe", C("transpose", psT[:, j4 * 128:(j4 + 1) * 128], qb_[:, 0:128], ident[:]),
                  r=[("qkb", t % 2), "ident"], w=["psT"])
                A("pe", C("transpose", psTb[:, j4 * 128:(j4 + 1) * 128], qb_[:, 128:256], ident[:]),
                  r=[("qkb", t % 2), "ident"], w=[TK[1]])
                ck(f"p4_{t}")
                if j4 == 3:
                    c4 = t // 4
                    A("act", C("copy", out=qT[b][:, c4 * 512:(c4 + 1) * 512], in_=psT[:, 0:512]), r=["psT"], w=[("qT", b, c4)])
                    A("dve", C("tensor_copy", out=kT[b][:, c4 * 512:(c4 + 1) * 512], in_=psTb[:, 0:512]), r=[TK[1]], w=[("kT", b, c4)])
                ck(f"p5_{t}")
            ck("proj")
            ei = 0
            si = 0
            for j in range(4):
                def acc(c, i):
                    if i < 3:
                        return psA[c][:, i * 129:(i + 1) * 129], ("psA", c)
                    return psA[2][:, c * 129:(c + 1) * 129], ("psA", 2)
                for kb in range(4 * j + 4):
                    nq0 = max(kb, 4 * j)
                    N = (4 * j + 4 - nq0) * 128
                    for c in range(2):
                        sbk = psS[si % 2]
                        sk = ("psS", si % 2)
                        si += 1
                        E = Et[ei % 4]
                        ek = ("E", ei % 4)
                        ei += 1
                        A("pe", C("matmul", sbk[:, 0:N], kT[b][64 * c:64 * c + 64, kb * 128:(kb + 1) * 128],
                            qT[b][64 * c:64 * c + 64, nq0 * 128:nq0 * 128 + N], start=True, stop=True),
                          r=[("kT", b, kb // 4), ("qT", b, j)], w=[sk])
                        A("act", C("activation", out=E[:, 0:N], in_=sbk[:, 0:N], func=AF.Exp, scale=0.125),
                          r=[sk], w=[ek])
                        if kb >= 4 * j:
                            A("dve", C("tensor_tensor", out=E[:, 0:128], in0=E[:, 0:128], in1=masks[:, 0, 128:256], op=ALU.mult),
                              r=[ek, "masks"], w=[ek])
                        for qb in range(nq0, 4 * j + 4):
                            ap_, ak = acc(c, qb - 4 * j)
                            A("pe", C("matmul", ap_, E[:, (qb - nq0) * 128:(qb - nq0 + 1) * 128], vh[b][:, kb, 0:129],
                                start=(kb == 0 and ((qb - 4 * j) == 0 or ((qb - 4 * j) == 3 and c == 0))), stop=(kb == qb), skip_group_check=True),
                              r=[ek, ("vh", b, kb)], w=[ak])
                for i in range(4):
                    qb = 4 * j + i
                    a0, k0 = acc(0, i)
                    a1, k1 = acc(1, i)
                    s_ = sm[i % 2]
                    sk_ = ("sm", i % 2)
                    o1_, o2_, on_ = o1[i % 2], o2[i % 2], onb[i % 2]
                    A("dve", C("reciprocal", out=s_[:, 0:1], in_=a0[:, 128:129]), r=[k0], w=[sk_])
                    A("dve", C("reciprocal", out=s_[:, 1:2], in_=a1[:, 128:129]), r=[k1], w=[sk_])
                    A("dve", C("tensor_tensor", out=s_[:, 1:2], in0=s_[:, 1:2], in1=lamv[:, 4:5], op=ALU.mult),
                      r=[sk_, "lamv"], w=[sk_])
                    A("act", C("activation", out=o1_[:], in_=a0[:, 0:128], func=AF.Copy, scale=s_[:, 0:1]),
                      r=[k0, sk_], w=[("o1", i % 2)])
                    A("dve", C("scalar_tensor_tensor", out=o2_[:], in0=a1[:, 0:128], scalar=s_[:, 1:2], in1=o1_[:], op0=ALU.mult, op1=ALU.add),
                      r=[k1, sk_, ("o1", i % 2)], w=[("o2", i % 2)])
                    A("act", C("activation", out=on_[:], in_=o2_[:], func=AF.Square, accum_out=s_[:, 2:3]), r=[("o2", i % 2)], w=[sk_, ("onb", i % 2)])
                    rsqrt(s_[:, 3:4], s_[:, 2:3], 128 * EPS, [sk_])
                    A("dve", C("scalar_tensor_tensor", out=on_[:], in0=o2_[:], scalar=s_[:, 3:4], in1=sgb[:], op0=ALU.mult, op1=ALU.mult),
                      r=[("o2", i % 2), sk_, "sgb"], w=[("onb", i % 2)])
                    A("pe", C("transpose", psT[:, i * 128:(i + 1) * 128], on_[:], ident[:]),
                      r=[("onb", i % 2), "ident"], w=["psT"])
                A("act", C("copy", out=oT[b][:, j * 512:(j + 1) * 512], in_=psT[:, 0:512]), r=["psT"], w=[("oT", b, j)])
            ck("attn")
            out_proj_add(oT[b], lambda t: [("oT", b, t // 4)], wos[b], ("wos", b))
            ck("oproj")

    def strided_tok(ap2d, r, c, n):
        return ap2d.rearrange("p (n b r) -> p n r b", b=128, r=r)[:, n, c, :]

    def kv_phase():
        sch.barrier()
        rmsnorm_to_hT(kv_norm)
        wk = kv_w.rearrange("(c p) n -> p c n", p=128)
        cv = Carver()
        wkv = [cv.take([128, NCH, 512], BF16) for _ in range(2)]
        kst = [cv.take([128, 4, S], BF16) for _ in range(2)]
        kb16 = [cv.take([128, 512], BF16) for _ in range(3)]
        tmp = dict(sq=cv.take([128, 512], F32), ya=cv.take([128, 512], F32), yb=cv.take([128, 512], F32),
                   rs=cv.take([128, 16], F32), key="tmpK")
        for cg in range(6):
            g, half = cg // 2, cg % 2
            b = cg % 2
            dma_pool(wkv[b][:], wk[:, :, cg * 512:(cg + 1) * 512], w=[("wkv", b)])
            if half == 0:
                dma_sp(gst[:, 0:64], kv_k_gain[g].partition_broadcast(128), r=["qkg"], w=["gst"])
                A("dve", C("tensor_scalar", out=qkg[:, 0:512].rearrange("p (h d) -> p h d", d=64),
                           in0=gst[:, 0:64].unsqueeze(1).to_broadcast([128, 8, 64]), scalar1=8.0, scalar2=None, op0=ALU.mult),
                  r=["gst"], w=["qkg"])
            for t in range(NT + 2):
                if t < NT:
                    pb = psP[t % 2]
                    pk = ("psP", t % 2)
                    for c in range(NCH):
                        A("pe", C("matmul", pb[:], hT[:, c, t * 128:(t + 1) * 128], wkv[b][:, c, :], start=(c == 0), stop=(c == NCH - 1)),
                          r=[("hT", t), ("wkv", b)], w=[pk])
                    qk_norm_rope(pb[:, 0:512], 8, t, kb16[t % 3][:], [pk], ("kb16", t % 3), tmp)
                if t >= 2:
                    tp = t - 2
                    kb_ = kb16[tp % 3]
                    hf = tp % 2
                    for pr in range(4):
                        A("pe", C("transpose", TB[hf][:, pr * 128:(pr + 1) * 128], kb_[:, pr * 128:(pr + 1) * 128], ident[:]),
                          r=[("kb16", tp % 3), "ident"], w=[TK[hf]])
                    src = TB[hf][:, 0:512].rearrange("p (a k) -> p a k", k=128)
                    A("act", C("copy", out=kst[b][:, :, tp * 128:(tp + 1) * 128], in_=src), r=[TK[hf]], w=[("kst", b, tp)])
            dma_sp(kscr[g, 4 * half:4 * half + 4].rearrange("a p s -> p a s"), kst[b][:, :, :],
                   r=[("kst", b, t) for t in range(NT)], w=[("kscr", g, 4 * half + pr) for pr in range(4)])
        sch.barrier()
        cv = Carver()
        wkv = [cv.take([128, NCH, 512], BF16) for _ in range(2)]
        vst = [cv.take([128, NT, 512], BF16) for _ in range(2)]
        allh = [("hT", t) for t in range(NT)]
        for cg in range(6):
            g, half = cg // 2, cg % 2
            r_ = GROUPS[g][1]
            nb = 16 // r_
            b = cg % 2
            dma_pool(wkv[b][:], wk[:, :, 3 * D + cg * 512: 3 * D + (cg + 1) * 512], w=[("wkv", b)])
            for pt in range(NT):
                c_, n_ = pt // nb, pt % nb
                pb = psP[pt % 2]
                pk = ("psP", pt % 2)
                for c in range(NCH):
                    A("pe", C("matmul", pb[:], strided_tok(hT[:, c, :], r_, c_, n_), wkv[b][:, c, :], start=(c == 0), stop=(c == NCH - 1)),
                      r=allh + [("wkv", b)], w=[pk])
                if pt % 2 == 0:
                    A("act", C("copy", out=vst[b][:, pt, :], in_=pb[:]), r=[pk], w=[("vst", b, pt)])
                else:
                    A("dve", C("tensor_copy", out=vst[b][:, pt, :], in_=pb[:]), r=[pk], w=[("vst", b, pt)])
            dma_sp(vscr[g, half], vst[b][:, :, :], r=[("vst", b, pt) for pt in range(NT)], w=[("vscr", g, half)])

    def layer_b(bl):
        cv = Carver()
        wsl = [cv.take([128, NCH, 384], BF16) for _ in range(2)]
        wos = cv.take([128, D], BF16)
        qT3 = cv.take([128, 3, S], BF16)
        kTg = [cv.take([128, S], BF16) for _ in range(2)]
        vaug = [cv.take([128, NT, 192], BF16) for _ in range(2)]
        accS = [cv.take([128, S], F32) for _ in range(2)]
        oT = cv.take([128, S], BF16)
        rcp = cv.take([128, 512], F32)
        qkb = [cv.take([128, 384], BF16) for _ in range(2)]
        Et = [cv.take([128, 512], BF16) for _ in range(3)]
        tmp = dict(sq=cv.take([128, 384], F32), ya=cv.take([128, 384], F32), yb=cv.take([128, 384], F32),
                   rs=cv.take([128, 16], F32), key="tmpB")
        rmsnorm_to_hT(b_norm[bl])
        dma_sp(gst[:, 0:192], b_q_gain[bl].rearrange("g d -> (g d)").partition_broadcast(128), r=["qkg"], w=["gst"])
        A("dve", C("tensor_scalar", out=qkg[:, 0:384].rearrange("p (g a d) -> p g a d", a=2, d=64),
                   in0=gst[:, 0:192].rearrange("p (g d) -> p g d", d=64).unsqueeze(2).to_broadcast([128, 3, 2, 64]),
                   scalar1=8.0, scalar2=None, op0=ALU.mult), r=["gst"], w=["qkg"])
        for bb in range(2):
            A("dve", C("memset", vaug[bb][:, :, 64:128], 1.0), w=[("vaug", bb, 0), ("vaug", bb, 1), ("vaugo", bb)])
        wq = b_w_q[bl].rearrange("(c p) n -> p c n", p=128)
        ei = 0
        si = 0
        oi = 0
        li = 0
        for j in range(8):
            b = j % 2
            kws = [("wsl", b, g) for g in range(3)]
            for g in range(3):
                dma_pool(wsl[b][:, :, g * 128:(g + 1) * 128], wq[:, :, g * D + j * 128: g * D + (j + 1) * 128], w=[kws[g]])
            dma_pool(wos[:], b_w_o[bl][j * 128:(j + 1) * 128, :], w=["wos"])
            for t in range(NT):
                pb = psP[t % 2]
                pk = ("psP", t % 2)
                for c in range(NCH):
                    A("pe", C("matmul", pb[:, 0:384], hT[:, c, t * 128:(t + 1) * 128], wsl[b][:, c, :], start=(c == 0), stop=(c == NCH - 1)),
                      r=[("hT", t)] + kws, w=[pk])
                qb_ = qkb[t % 2]
                qk_norm_rope(pb[:, 0:384], 6, t, qb_[:], [pk], ("qkb", t % 2), tmp)
                hf = t % 2
                for g in range(3):
                    A("pe", C("transpose", TB[hf][:, g * 128:(g + 1) * 128], qb_[:, g * 128:(g + 1) * 128], ident[:]),
                      r=[("qkb", t % 2), "ident"], w=[TK[hf]])
                src = TB[hf][:, 0:384].rearrange("p (a k) -> p a k", k=128)
                if t % 2 == 0:
                    A("act", C("copy", out=qT3[:, :, t * 128:(t + 1) * 128], in_=src), r=[TK[hf]], w=[("qT3", t)])
                else:
                    A("dve", C("tensor_copy", out=qT3[:, :, t * 128:(t + 1) * 128], in_=src), r=[TK[hf]], w=[("qT3", t)])
            allq = [("qT3", t) for t in range(NT)]
            for g in range(3):
                r_ = GROUPS[g][1]
                nb = 16 // r_
                bb = li % 2
                li += 1
                half, hh = j // 4, 2 * (j % 4)
                dma_sp(kTg[bb][:], kscr[g, j], r=[("kscr", g, j)], w=[("kTg", bb)])
                dma_sp(vaug[bb][:, :, 0:64], vscr[g, half][:, :, hh * 64:(hh + 1) * 64], r=[("vscr", g, half)], w=[("vaug", bb, 0)])
                dma_sp(vaug[bb][:, :, 128:192], vscr[g, half][:, :, (hh + 1) * 64:(hh + 2) * 64], r=[("vscr", g, half)], w=[("vaug", bb, 1)])
                vkeys = [("vaug", bb, 0), ("vaug", bb, 1), ("vaugo", bb)]
                for X in range(2):
                    rows = slice(64 * X, 64 * X + 64)
                    vc = slice(64 * X, 64 * X + 128)
                    accv = accS[X].rearrange("p (n b r) -> p r n b", b=128, r=r_)
                    for quad in range(4):
                        ob = psA[oi % 3]
                        ok = ("psA", oi % 3)
                        oi += 1
                        for h2 in range(2):
                            blks = (4 * quad + 2 * h2, 4 * quad + 2 * h2 + 1)
                            sbk = psS[si % 2]
                            sk = ("psS", si % 2)
                            si += 1
                            E = Et[ei % 3]
                            ek = ("E", ei % 3)
                            ei += 1
                            for ub, blk in enumerate(blks):
                                c_, n_ = blk // nb, blk % nb
                                qap = strided_tok(qT3[rows, g, :], r_, c_, n_)
                                for u2, kn in enumerate((n_ - 1 if n_ > 0 else n_, n_)):
                                    u = 2 * ub + u2
                                    A("pe", C("matmul", sbk[:, u * 128:(u + 1) * 128], strided_tok(kTg[bb][rows, :], r_, c_, kn), qap,
                                              start=True, stop=True),
                                      r=[("kTg", bb)] + allq, w=[sk])
                            A("act", C("activation", out=E[:], in_=sbk[:], func=AF.Exp, scale=0.125), r=[sk], w=[ek])
                            n0, n1 = blks[0] % nb, blks[1] % nb
                            mi = 0 if (n0 > 0 and n1 > 0) else (1 if n1 > 0 else 2)
                            A("dve", C("tensor_tensor", out=E[:], in0=E[:], in1=masks[:, mi, :], op=ALU.mult), r=[ek, "masks"], w=[ek])
                            for ub, blk in enumerate(blks):
                                n_ = blk % nb
                                slot = blk - 4 * quad
                                oap = ob[:, slot * 128:(slot + 1) * 128]
                                if n_ > 0:
                                    A("pe", C("matmul", oap, vaug[bb][:, blk - 1, vc], E[:, (2 * ub) * 128:(2 * ub + 1) * 128],
                                              start=True, stop=False), r=[ek] + vkeys, w=[ok])
                                A("pe", C("matmul", oap, vaug[bb][:, blk, vc], E[:, (2 * ub + 1) * 128:(2 * ub + 2) * 128],
                                          start=(n_ == 0), stop=True), r=[ek] + vkeys, w=[ok])
                        if g == 0:
                            dst = accv[:, 0, 4 * quad:4 * quad + 4, :]
                        elif g == 1:
                            dst = accv[:, quad, 0:4, :]
                        else:
                            dst = accv[:, 4 * quad:4 * quad + 4, 0, :]
                        srcv = ob[:].rearrange("p (a k) -> p a k", k=128)
                        if g == 0:
                            A("act", C("copy", out=dst, in_=srcv), r=[ok], w=[("acc", X)])
                        else:
                            A("dve", C("tensor_tensor", out=dst, in0=srcv, in1=dst, op=ALU.add), r=[ok, ("acc", X)], w=[("acc", X)])
            for X in range(2):
                rows = slice(64 * X, 64 * X + 64)
                for ck in range(4):
                    ob = psA[oi % 3]
                    ok = ("psA", oi % 3)
                    oi += 1
                    A("pe", C("matmul", ob[:], swf[:], accS[X][:, ck * 512:(ck + 1) * 512], start=True, stop=True),
                      r=[("acc", X), "swf"], w=[ok])
                    A("dve", C("reciprocal", out=rcp[rows, :], in_=ob[rows, :]), r=[ok], w=[("rcp", X)])
                    A("dve", C("tensor_tensor", out=oT[rows, ck * 512:(ck + 1) * 512], in0=accS[X][rows, ck * 512:(ck + 1) * 512],
                               in1=rcp[rows, :], op=ALU.mult), r=[("rcp", X), ("acc", X)], w=[("oTb", X, ck)])
            out_proj_add(oT, lambda t: [("oTb", 0, t // 4), ("oTb", 1, t // 4)], wos, "wos")

    def merge(ga, na, gb, nb):
        ia = ib = 0
        da = db = False
        while not (da and db):
            if not da and (db or ia * nb <= ib * na):
                try:
                    next(ga)
                except StopIteration:
                    da = True
                ia += 1
            else:
                try:
                    next(gb)
                except StopIteration:
                    db = True
                ib += 1

    def drain(g):
        for _ in g:
            pass

    def out_proj_gen(oT_ap, okeys, wo_ap, wkey):
        i = 0
        for t in range(NT):
            for h in range(2):
                pb = psA[i % 3]
                pk = ("psA", i % 3)
                i += 1
                A("pe", C("matmul", pb[:], oT_ap[:, t * 128:(t + 1) * 128], wo_ap[:, h * 512:(h + 1) * 512], start=True, stop=True),
                  r=okeys(t) + [wkey], w=[pk])
                A("dve", C("tensor_tensor", out=xres[:, t, h * 512:(h + 1) * 512], in0=pb[:], in1=xres[:, t, h * 512:(h + 1) * 512], op=ALU.add),
                  r=[pk, ("x", t)], w=[("x", t)])
            yield

    def mergeN(gens, weights):
        n = len(gens)
        prog = [0] * n
        done = [False] * n
        while not all(done):
            k = min((i for i in range(n) if not done[i]), key=lambda i: prog[i] / weights[i])
            try:
                next(gens[k])
            except StopIteration:
                done[k] = True
            prog[k] += 1

    def layer_a2(l):
        cv = Carver()
        wsl = [cv.take([128, NCH, 384], BF16) for _ in range(2)]
        wos = [cv.take([128, D], BF16) for _ in range(2)]
        qT = [cv.take([128, S], BF16) for _ in range(2)]
        kT = [cv.take([128, S], BF16) for _ in range(2)]
        vh = [cv.take([128, NT, 130], BF16) for _ in range(2)]
        oT = [cv.take([128, S], BF16) for _ in range(4)]
        qkb = [cv.take([128, 256], BF16) for _ in range(3)]
        yps = [cv.take([128, 384], F32) for _ in range(2)]
        Et = [cv.take([128, 512], BF16) for _ in range(4)]
        accc = [[cv.take([128, 387], F32) for _ in range(2)] for _ in range(2)]
        acc3 = [cv.take([128, 258], F32) for _ in range(2)]
        tmp = dict(sq=cv.take([128, 512], F32), ya=cv.take([128, 512], F32), yb=cv.take([128, 512], F32),
                   rs=cv.take([128, 16], F32), key="tmpA")
        o1 = [cv.take([128, 128], F32) for _ in range(2)]
        o2 = [cv.take([128, 128], F32) for _ in range(2)]
        onb = [cv.take([128, 128], BF16) for _ in range(2)]
        sm = [cv.take([128, 8], F32) for _ in range(2)]
        lam_init = 0.8 - 0.6 * math.exp(-0.3 * l)

        rmsnorm_to_hT(a_norm[l])
        for j in range(2):
            dma_sp(qkg[:, j * 64:(j + 1) * 64], a_q_gain[l].partition_broadcast(128), r=["qkg"], w=[("qkgp", j)])
            dma_sp(qkg[:, 128 + j * 64:128 + (j + 1) * 64], a_k_gain[l].partition_broadcast(128), r=["qkg"], w=[("qkgp", 2 + j)])
        A("dve", C("tensor_scalar", out=qkg[:, 0:256], in0=qkg[:, 0:256], scalar1=8.0, scalar2=None, op0=ALU.mult),
          r=[("qkgp", j) for j in range(4)], w=["qkg"] + [("qkgp", j) for j in range(4)])
        load_bcast(sgb[:], a_sub_gain[l], 128, "sgb", scale=math.sqrt(128.0) * (1.0 - lam_init))
        for j in range(4):
            dma_sp(lamt[:, j, :], a_lam[j][l].partition_broadcast(128), r=["lamt"], w=[("lamtp", j)])
        A("dve", C("tensor_tensor", out=lamt[:, 0, :], in0=lamt[:, 0, :], in1=lamt[:, 1, :], op=ALU.mult),
          r=[("lamtp", j) for j in range(4)], w=["lamt"] + [("lamtp", j) for j in range(4)])
        A("dve", C("tensor_tensor", out=lamt[:, 2, :], in0=lamt[:, 2, :], in1=lamt[:, 3, :], op=ALU.mult), r=["lamt"], w=["lamt"])
        A("dve", C("tensor_reduce", out=lamv[:, 0:1], in_=lamt[:, 0, :], axis=AX.X, op=ALU.add), r=["lamt"], w=["lamv"])
        A("dve", C("tensor_reduce", out=lamv[:, 1:2], in_=lamt[:, 2, :], axis=AX.X, op=ALU.add), r=["lamt"], w=["lamv"])
        A("act", C("activation", out=lamv[:, 2:4], in_=lamv[:, 0:2], func=AF.Exp), r=["lamv"], w=["lamv"])
        A("dve", C("tensor_scalar", out=lamv[:, 5:6], in0=lamv[:, 2:3], scalar1=-1.0, scalar2=float(-lam_init),
                   op0=ALU.mult, op1=ALU.add), r=["lamv"], w=["lamv"])
        A("dve", C("tensor_tensor", out=lamv[:, 4:5], in0=lamv[:, 5:6], in1=lamv[:, 3:4], op=ALU.add), r=["lamv"], w=["lamv"])
        for bb in range(2):
            A("dve", C("memset", vh[bb][:, :, 128:130], 1.0), w=[("vh", bb, c4) for c4 in range(NT)])
        wq = a_w_qkv[l].rearrange("(c p) n -> p c n", p=128)
        PP, PPK = psP[0], ("psP", 0)

        def proj_gen(hd):
            b = hd % 2
            kws = [("wsl", b, j) for j in range(3)]
            for j in range(3):
                dma_pool(wsl[b][:, :, j * 128:(j + 1) * 128], wq[:, :, j * D + hd * 128: j * D + (hd + 1) * 128], w=[kws[j]])
            for t in range(NT + 2):
                if t >= 2:
                    tp = t - 2
                    qb_ = qkb[tp % 3]
                    A("pe", C("transpose", psT[:, 0:128], qb_[:, 0:128], ident[:]), r=[("qkb", tp % 3), "ident"], w=["psT"])
                    A("pe", C("transpose", psT[:, 128:256], qb_[:, 128:256], ident[:]), r=[("qkb", tp % 3), "ident"], w=["psT"])
                    A("act", C("copy", out=qT[b][:, tp * 128:(tp + 1) * 128], in_=psT[:, 0:128]), r=["psT"], w=[("qT", b, tp // 4)])
                    A("dve", C("tensor_copy", out=kT[b][:, tp * 128:(tp + 1) * 128], in_=psT[:, 128:256]), r=["psT"], w=[("kT", b, tp // 4)])
                    yield
                if t < NT:
                    for c in range(NCH):
                        A("pe", C("matmul", PP[:, 0:384], hT[:, c, t * 128:(t + 1) * 128], wsl[b][:, c, :],
                                  start=(c == 0), stop=(c == NCH - 1)), r=[("hT", t)] + kws, w=[PPK])
                    y_ = yps[t % 2]
                    yk = ("yps", t % 2)
                    A("dve", C("tensor_copy", out=y_[:], in_=PP[:, 0:384]), r=[PPK], w=[yk])
                    A("dve", C("tensor_copy", out=vh[b][:, t, 0:128], in_=y_[:, 256:384]), r=[yk], w=[("vh", b, t)])
                    for _ in qk_norm_rope_g(y_[:, 0:256], 4, t, qkb[t % 3][:], [yk], ("qkb", t % 3), tmp):
                        yield

        def fin_gen(hd, j, buf):
            ob_ = hd % 4
            for i in range(4):
                if i < 3:
                    a0 = accc[buf][0][:, i * 129:(i + 1) * 129]
                    a1 = accc[buf][1][:, i * 129:(i + 1) * 129]
                    k0, k1 = ("accc", buf, 0), ("accc", buf, 1)
                else:
                    a0 = acc3[buf][:, 0:129]
                    a1 = acc3[buf][:, 129:258]
                    k0 = k1 = ("acc3", buf)
                s_ = sm[i % 2]
                sk_ = ("sm", i % 2)
                o1_, o2_, on_ = o1[i % 2], o2[i % 2], onb[i % 2]
                A("dve", C("reciprocal", out=s_[:, 0:1], in_=a0[:, 128:129]), r=[k0], w=[sk_])
                A("dve", C("reciprocal", out=s_[:, 1:2], in_=a1[:, 128:129]), r=[k1], w=[sk_])
                A("dve", C("tensor_tensor", out=s_[:, 1:2], in0=s_[:, 1:2], in1=lamv[:, 4:5], op=ALU.mult), r=[sk_, "lamv"], w=[sk_])
                yield
                A("act", C("activation", out=o1_[:], in_=a0[:, 0:128], func=AF.Copy, scale=s_[:, 0:1]), r=[k0, sk_], w=[("o1", i % 2)])
                yield
                A("dve", C("scalar_tensor_tensor", out=o2_[:], in0=a1[:, 0:128], scalar=s_[:, 1:2], in1=o1_[:], op0=ALU.mult, op1=ALU.add),
                  r=[k1, sk_, ("o1", i % 2)], w=[("o2", i % 2)])
                yield
                A("act", C("activation", out=on_[:], in_=o2_[:], func=AF.Square, accum_out=s_[:, 2:3]), r=[("o2", i % 2)], w=[sk_, ("onb", i % 2)])
                yield
                rsqrt(s_[:, 3:4], s_[:, 2:3], 128 * EPS, [sk_])
                yield
                A("dve", C("scalar_tensor_tensor", out=on_[:], in0=o2_[:], scalar=s_[:, 3:4], in1=sgb[:], op0=ALU.mult, op1=ALU.mult),
                  r=[("o2", i % 2), sk_, "sgb"], w=[("onb", i % 2)])
                yield
                A("pe", C("transpose", psT[:, 512:640], on_[:], ident[:]), r=[("onb", i % 2), "ident"], w=["psT"])
                A("act", C("copy", out=oT[ob_][:, (4 * j + i) * 128:(4 * j + i + 1) * 128], in_=psT[:, 512:640]), r=["psT"], w=[("oT", ob_, j)])
                yield

        def attn_gen(hd):
            b = hd % 2
            ei = [0]
            si = [0]

            def acc(c, i):
                if i < 3:
                    return psA[c][:, i * 129:(i + 1) * 129], ("psA", c)
                return psA[2][:, c * 129:(c + 1) * 129], ("psA", 2)

            pend = None
            for j in range(4):
                steps = [(kb, c) for kb in range(4 * j + 4) for c in range(2)]
                info = {}

                def issue_S(idx):
                    kb, c = steps[idx]
                    nq0 = max(kb, 4 * j)
                    N = (4 * j + 4 - nq0) * 128
                    sbk, sk = SB3[si[0] % 3]
                    si[0] += 1
                    diag = kb >= 4 * j
                    A("pe", C("matmul", sbk[:, 0:N], kT[b][64 * c:64 * c + 64, kb * 128:(kb + 1) * 128],
                              qT[b][64 * c:64 * c + 64, nq0 * 128:nq0 * 128 + N], start=True, stop=not diag, skip_group_check=True),
                      r=[("kT", b, kb // 4), ("qT", b, j)], w=[sk])
                    if diag:
                        A("pe", C("matmul", sbk[:, 0:128], negi[:], masks[:, 0, 128:256], start=False, stop=True, skip_group_check=True),
                          r=["negi", "masks"], w=[sk])
                    info[idx] = (sbk, sk, nq0, N)

                issue_S(0)
                issue_S(1)
                for idx, (kb, c) in enumerate(steps):
                    sbk, sk, nq0, N = info[idx]
                    E = Et[ei[0] % 4]
                    ek = ("E", ei[0] % 4)
                    ei[0] += 1
                    A("act", C("activation", out=E[:, 0:N], in_=sbk[:, 0:N], func=AF.Exp, scale=0.125), r=[sk], w=[ek])
                    if idx + 2 < len(steps):
                        issue_S(idx + 2)
                    for qb in range(nq0, 4 * j + 4):
                        ap_, ak = acc(c, qb - 4 * j)
                        i_ = qb - 4 * j
                        A("pe", C("matmul", ap_, E[:, (qb - nq0) * 128:(qb - nq0 + 1) * 128], vh[b][:, kb, 0:129],
                                  start=(kb == 0 and (i_ == 0 or (i_ == 3 and c == 0))), stop=(kb == qb), skip_group_check=True),
                          r=[ek, ("vh", b, kb)], w=[ak])
                    if pend is not None:
                        try:
                            next(pend)
                            next(pend)
                        except StopIteration:
                            pend = None
                    yield
                if pend is not None:
                    for _ in pend:
                        yield
                    pend = None
                buf = j % 2
                A("act", C("copy", out=accc[buf][0][:], in_=psA[0][:, 0:387]), r=[("psA", 0)], w=[("accc", buf, 0)])
                A("dve", C("tensor_copy", out=accc[buf][1][:], in_=psA[1][:, 0:387]), r=[("psA", 1)], w=[("accc", buf, 1)])
                A("act", C("copy", out=acc3[buf][:], in_=psA[2][:, 0:258]), r=[("psA", 2)], w=[("acc3", buf)])
                pend = fin_gen(hd, j, buf)
                yield
            for _ in pend:
                yield

        def out_pair_gen(pr):
            hs = (2 * pr, 2 * pr + 1)
            for q, h_ in enumerate(hs):
                dma_pool(wos[q][:], a_w_o[l][h_ * 128:(h_ + 1) * 128, :], w=[("wos", q)])
            yield
            i = 0
            for t in range(NT):
                for h in range(2):
                    pb, pk = psP[0], ("psP", 0)
                    for q, h_ in enumerate(hs):
                        A("pe", C("matmul", pb[:], oT[h_ % 4][:, t * 128:(t + 1) * 128], wos[q][:, h * 512:(h + 1) * 512],
                                  start=(q == 0), stop=(q == 1)), r=[("oT", h_ % 4, t // 4), ("wos", q)], w=[pk])
                    A("dve", C("tensor_tensor", out=xres[:, t, h * 512:(h + 1) * 512], in0=pb[:], in1=xres[:, t, h * 512:(h + 1) * 512], op=ALU.add),
                      r=[pk, ("x", t)], w=[("x", t)])
                    i += 1
                    yield

        drain(proj_gen(0))
        pending_out = None
        for hd in range(8):
            gens = [attn_gen(hd)]
            wts = [125.0]
            if hd + 1 < 8:
                gens.append(proj_gen(hd + 1))
                wts.append(82.0)
            if pending_out is not None:
                gens.append(pending_out)
                wts.append(33.0)
                pending_out = None
            mergeN(gens, wts)
            if hd % 2 == 1:
                pending_out = out_pair_gen(hd // 2)
        drain(pending_out)

    def layer_b2(bl):
        cv = Carver()
        wsl = cv.take([128, NCH, 384], BF16)
        wos = cv.take([128, D], BF16)
        qT3 = [cv.take([128, 3, S], BF16) for _ in range(2)]
        kTg = [cv.take([128, S], BF16) for _ in range(2)]
        vaug = [cv.take([128, NT, 192], BF16) for _ in range(2)]
        accS = [cv.take([128, S], F32) for _ in range(2)]
        oT = cv.take([128, S], BF16)
        rcp = cv.take([128, 512], F32)
        qkb = [cv.take([128, 384], BF16) for _ in range(3)]
        yps = [cv.take([128, 384], F32)] * 2
        Et = [cv.take([128, 512], BF16) for _ in range(3)]
        tmp = dict(sq=cv.take([128, 384], F32), ya=cv.take([128, 384], F32), yb=cv.take([128, 384], F32),
                   rs=cv.take([128, 16], F32), key="tmpB")
        rmsnorm_to_hT(b_norm[bl])
        dma_sp(gst[:, 0:192], b_q_gain[bl].rearrange("g d -> (g d)").partition_broadcast(128), r=["qkg"], w=["gst"])
        A("dve", C("tensor_scalar", out=qkg[:, 0:384].rearrange("p (g a d) -> p g a d", a=2, d=64),
                   in0=gst[:, 0:192].rearrange("p (g d) -> p g d", d=64).unsqueeze(2).to_broadcast([128, 3, 2, 64]),
                   scalar1=8.0, scalar2=None, op0=ALU.mult), r=["gst"], w=["qkg"])
        for bb in range(2):
            A("dve", C("memset", vaug[bb][:, :, 64:128], 1.0), w=[("vaug", bb, 0), ("vaug", bb, 1), ("vaugo", bb)])
        wq = b_w_q[bl].rearrange("(c p) n -> p c n", p=128)
        PP, PPK = psP[0], ("psP", 0)
        cnt_ = dict(ei=0, si=0, oi=0, li=0)

        def proj_gen(j):
            b = j % 2
            kws = [("wsl", g) for g in range(3)]
            for g in range(3):
                dma_pool(wsl[:, :, g * 128:(g + 1) * 128], wq[:, :, g * D + j * 128: g * D + (j + 1) * 128], w=[kws[g]])
            for t in range(NT + 2):
                if t >= 2:
                    tp = t - 2
                    qb_ = qkb[tp % 3]
                    for g in range(3):
                        A("pe", C("transpose", psT[:, g * 128:(g + 1) * 128], qb_[:, g * 128:(g + 1) * 128], ident[:]),
                          r=[("qkb", tp % 3), "ident"], w=["psT"])
                    src = psT[:, 0:384].rearrange("p (a k) -> p a k", k=128)
                    A("act", C("copy", out=qT3[b][:, :, tp * 128:(tp + 1) * 128], in_=src), r=["psT"], w=[("qT3", b, tp)])
                    yield
                if t < NT:
                    for c in range(NCH):
                        A("pe", C("matmul", PP[:, 0:384], hT[:, c, t * 128:(t + 1) * 128], wsl[:, c, :], start=(c == 0), stop=(c == NCH - 1)),
                          r=[("hT", t)] + kws, w=[PPK])
                    y_ = yps[t % 2]
                    yk = ("yps", 0)
                    A("act", C("copy", out=y_[:], in_=PP[:, 0:384]), r=[PPK], w=[yk])
                    for _ in qk_norm_rope_g(y_[:, 0:384], 6, t, qkb[t % 3][:], [yk], ("qkb", t % 3), tmp):
                        yield

        def attn_gen(j):
            b = j % 2
            allq = [("qT3", b, t) for t in range(NT)]
            dma_pool(wos[:], b_w_o[bl][j * 128:(j + 1) * 128, :], w=["wos"])
            for g in range(3):
                r_ = GROUPS[g][1]
                nb = 16 // r_
                bb = cnt_["li"] % 2
                cnt_["li"] += 1
                half, hh = j // 4, 2 * (j % 4)
                dma_sp(kTg[bb][:], kscr[g, j], r=[("kscr", g, j)], w=[("kTg", bb)])
                dma_sp(vaug[bb][:, :, 0:64], vscr[g, half][:, :, hh * 64:(hh + 1) * 64], r=[("vscr", g, half)], w=[("vaug", bb, 0)])
                dma_sp(vaug[bb][:, :, 128:192], vscr[g, half][:, :, (hh + 1) * 64:(hh + 2) * 64], r=[("vscr", g, half)], w=[("vaug", bb, 1)])
                vkeys = [("vaug", bb, 0), ("vaug", bb, 1), ("vaugo", bb)]
                batches = [(X, quad, h2) for X in range(2) for quad in range(4) for h2 in range(2)]
                info = {}

                def issue_S(idx):
                    X, quad, h2 = batches[idx]
                    rows = slice(64 * X, 64 * X + 64)
                    blks = (4 * quad + 2 * h2, 4 * quad + 2 * h2 + 1)
                    sbk, sk = SB3[cnt_["si"] % 3]
                    cnt_["si"] += 1
                    for ub, blk in enumerate(blks):
                        c_, n_ = blk // nb, blk % nb
                        qap = strided_tok(qT3[b][rows, g, :], r_, c_, n_)
                        for u2, kn in enumerate((n_ - 1 if n_ > 0 else n_, n_)):
                            u = 2 * ub + u2
                            A("pe", C("matmul", sbk[:, u * 128:(u + 1) * 128], strided_tok(kTg[bb][rows, :], r_, c_, kn), qap,
                                      start=(u == 0), stop=False, skip_group_check=True), r=[("kTg", bb)] + allq, w=[sk])
                    n0, n1 = blks[0] % nb, blks[1] % nb
                    mi = 0 if (n0 > 0 and n1 > 0) else (1 if n1 > 0 else 2)
                    A("pe", C("matmul", sbk[:], negi[:], masks[:, mi, :], start=False, stop=True, skip_group_check=True),
                      r=["negi", "masks"], w=[sk])
                    info[idx] = (sbk, sk, blks)

                issue_S(0)
                issue_S(1)
                ob = ok = None
                for idx, (X, quad, h2) in enumerate(batches):
                    sbk, sk, blks = info[idx]
                    vc = slice(64 * X, 64 * X + 128)
                    if h2 == 0:
                        ob = psA[cnt_["oi"] % 3]
                        ok = ("psA", cnt_["oi"] % 3)
                        cnt_["oi"] += 1
                    E = Et[cnt_["ei"] % 3]
                    ek = ("E", cnt_["ei"] % 3)
                    cnt_["ei"] += 1
                    A("act", C("activation", out=E[:], in_=sbk[:], func=AF.Exp, scale=0.125), r=[sk], w=[ek])
                    if idx + 2 < len(batches):
                        issue_S(idx + 2)
                    for ub, blk in enumerate(blks):
                        n_ = blk % nb
                        slot = blk - 4 * quad
                        oap = ob[:, slot * 128:(slot + 1) * 128]
                        if n_ > 0:
                            A("pe", C("matmul", oap, vaug[bb][:, blk - 1, vc], E[:, (2 * ub) * 128:(2 * ub + 1) * 128],
                                      start=True, stop=False), r=[ek] + vkeys, w=[ok])
                        A("pe", C("matmul", oap, vaug[bb][:, blk, vc], E[:, (2 * ub + 1) * 128:(2 * ub + 2) * 128],
                                  start=(n_ == 0), stop=True), r=[ek] + vkeys, w=[ok])
                    if h2 == 1:
                        accv = accS[X].rearrange("p (n b r) -> p r n b", b=128, r=r_)
                        if g == 0:
                            dst = accv[:, 0, 4 * quad:4 * quad + 4, :]
                        elif g == 1:
                            dst = accv[:, quad, 0:4, :]
                        else:
                            dst = accv[:, 4 * quad:4 * quad + 4, 0, :]
                        srcv = ob[:].rearrange("p (a k) -> p a k", k=128)
                        if g == 0:
                            A("act", C("copy", out=dst, in_=srcv), r=[ok], w=[("acc", X)])
                        else:
                            A("dve", C("tensor_tensor", out=dst, in0=srcv, in1=dst, op=ALU.add), r=[ok, ("acc", X)], w=[("acc", X)])
                    yield
            for X in range(2):
                rows = slice(64 * X, 64 * X + 64)
                for ck in range(4):
                    ob = psA[cnt_["oi"] % 3]
                    ok = ("psA", cnt_["oi"] % 3)
                    cnt_["oi"] += 1
                    A("pe", C("matmul", ob[:], swf[:], accS[X][:, ck * 512:(ck + 1) * 512], start=True, stop=True),
                      r=[("acc", X), "swf"], w=[ok])
                    A("dve", C("reciprocal", out=rcp[rows, :], in_=ob[rows, :]), r=[ok], w=[("rcp", X)])
                    A("dve", C("tensor_tensor", out=oT[rows, ck * 512:(ck + 1) * 512], in0=accS[X][rows, ck * 512:(ck + 1) * 512],
                               in1=rcp[rows, :], op=ALU.mult), r=[("rcp", X), ("acc", X)], w=[("oTb", X, ck)])
                    yield
            for _ in out_proj_gen(oT, lambda t: [("oTb", 0, t // 4), ("oTb", 1, t // 4)], wos, "wos"):
                yield

        drain(proj_gen(0))
        for j in range(8):
            if j + 1 < 8:
                merge(attn_gen(j), 72, proj_gen(j + 1), 82)
            else:
                drain(attn_gen(j))

    def mlp(l):
        cv = Carver()
        uT = cv.take([128, 32, 512], BF16)
        wup = [cv.take([128, NCH, 512], BF16) for _ in range(2)]
        wdn = [cv.take([128, 4, 512], BF16) for _ in range(4)]
        rl = [cv.take([128, 512], BF16) for _ in range(2)]
        rmsnorm_to_hT(m_norm[l])
        wu = m_w_up[l].rearrange("(c p) n -> p c n", p=128)
        wd = m_w_down[l].rearrange("(f p) n -> p f n", p=128)
        iu = 0
        idn = 0
        for tc4 in range(4):
            toks = slice(tc4 * 512, (tc4 + 1) * 512)
            for fg in range(8):
                wb = wup[iu % 2]
                wk = ("wup", iu % 2)
                iu += 1
                dma_pool(wb[:], wu[:, :, fg * 512:(fg + 1) * 512], w=[wk])
                for f4 in range(4):
                    f = fg * 4 + f4
                    pb = psP[f % 2]
                    pk = ("psP", f % 2)
                    for c in range(NCH):
                        A("pe", C("matmul", pb[:], wb[:, c, f4 * 128:(f4 + 1) * 128], hT[:, c, toks],
                                                                             start=(c == 0), stop=(c == NCH - 1)),
                          r=[wk] + [("hT", 4 * tc4 + q) for q in range(4)], w=[pk])
                    r_ = rl[f % 2]
                    A("act", C("activation", out=r_[:], in_=pb[:], func=AF.Relu), r=[pk], w=[("rl", f % 2)])
                    A("dve", C("tensor_tensor", out=uT[:, f, :], in0=r_[:], in1=r_[:], op=ALU.mult),
                      r=[("rl", f % 2)], w=[("uT", f)])
            for h in range(2):
                banks = [(psA[0], ("psA", 0)), (psA[1], ("psA", 1)), (psA[2], ("psA", 2)), (psS[0], ("psS", 0))]
                for fg in range(8):
                    wb = wdn[idn % 4]
                    wk = ("wdn", idn % 4)
                    idn += 1
                    dma_pool(wb[:], wd[:, fg * 4:(fg + 1) * 4, h * 512:(h + 1) * 512], w=[wk])
                    for q in range(4):
                        pb, pk = banks[q]
                        for f4 in range(4):
                            f = fg * 4 + f4
                            A("pe", C("matmul", pb[:], uT[:, f, q * 128:(q + 1) * 128], wb[:, f4, :],
                                                                                     start=(f == 0), stop=(f == 31)),
                              r=[("uT", f), wk], w=[pk])
                for q in range(4):
                    pb, pk = banks[q]
                    t = 4 * tc4 + q
                    A("dve", C("tensor_tensor", out=xres[:, t, h * 512:(h + 1) * 512], in0=pb[:],
                                                                        in1=xres[:, t, h * 512:(h + 1) * 512], op=ALU.add),
                      r=[pk, ("x", t)], w=[("x", t)])

    kv_done = False
    try:
      ck("load")
      for l in layers:
        sch.barrier()
        if l < 2:
            layer_a2(l)
        else:
            if not kv_done:
                kv_phase()
                kv_done = True
                sch.barrier()
            layer_b2(l - 2)
        ck("mixer")
        sch.barrier()
        mlp(l)
    except StopBuild:
        pass
    sch.barrier()
    yv = y_out.rearrange("(t p) d -> p t d", p=128)
    outs = []
    for g4 in range(4):
        outs.append(dma_sp(yv[:, 4 * g4:4 * g4 + 4, :], xres[:, 4 * g4:4 * g4 + 4, :], r=[("x", t) for t in range(4 * g4, 4 * g4 + 4)]))
    A("sp", None, after=outs, name="final")

    sch.finalize()
    with nc.Block() as block:
        @block.tensor
        def _(e):
            sch.emit("pe", e, sems, dsems)

        @block.scalar
        def _(e):
            sch.emit("act", e, sems, dsems)

        @block.vector
        def _(e):
            sch.emit("dve", e, sems, dsems)

        @block.gpsimd
        def _(e):
            sch.emit("pool", e, sems, dsems)

        @block.sync
        def _(e):
            sch.emit("sp", e, sems, dsems)
    es.close()
    return nc


_NC_CACHE = {}


def kernel(**inputs):
    names = ["a_norm", "a_w_qkv", "a_q_gain", "a_k_gain", "a_lam_q1", "a_lam_k1", "a_lam_q2", "a_lam_k2", "a_sub_gain",
             "a_w_o", "kv_norm", "kv_w", "kv_k_gain", "b_norm", "b_w_q", "b_q_gain", "b_w_o", "m_norm", "m_w_up", "m_w_down"]
    shared = {n: np.ascontiguousarray(np.asarray(inputs[n], dtype=np.float32)) for n in names}
    shared.update(_consts())
    x = np.asarray(inputs["x"], dtype=np.float32)
    nc = build()
    in_maps = []
    for i in range(8):
        m = dict(shared)
        m["x"] = np.ascontiguousarray(x[i])
        in_maps.append(m)
    res = run_bass_kernel_spmd(nc, in_maps, core_ids=list(range(8)))
    return np.stack([np.asarray(r["y"], dtype=np.float32) for r in res.results], axis=0)
```

```python
import math
from contextlib import ExitStack
import numpy as np
import concourse.bass as bass
import concourse.mybir as mybir
from concourse.bass_utils import run_bass_kernel_spmd

F32 = mybir.dt.float32
BF16 = mybir.dt.bfloat16
AF = mybir.ActivationFunctionType
ALU = mybir.AluOpType
AX = mybir.AxisListType

S = 2048
D = 1024
NT = 16
NCH = 8
DFF = 4096
EPS = 1e-6
NSEM_DMA = 8
GROUPS = ((128, 1), (512, 4), (2048, 16))


class Op:
    __slots__ = ("eng", "fn", "deps", "dma", "idx", "needs_inc", "ord", "sem_i", "sem_k", "name")

    def __init__(self, eng, fn, dma, name):
        self.eng = eng
        self.fn = fn
        self.dma = dma
        self.deps = set()
        self.needs_inc = False
        self.ord = 0
        self.sem_i = 0
        self.sem_k = 0
        self.name = name


def _is_psum(k):
    return k == "psT" or (isinstance(k, tuple) and k[0] in ("psP", "psS", "psA"))


class Sched:
    ENGS = ("pe", "act", "dve", "pool", "sp")

    def __init__(self):
        self.ops = {e: [] for e in self.ENGS}
        self.lastw = {}
        self.readers = {}
        self.ndma = {e: 0 for e in self.ENGS}
        self.dma_by_sem = {}

    def add(self, eng, fn, r=(), w=(), after=(), dma=False, name=""):
        op = Op(eng, fn, dma, name)
        deps = op.deps
        pr = [k for k in r if _is_psum(k)]
        if pr:
            w = list(w) + [k for k in pr if k not in w]
        for k in r:
            lw = self.lastw.get(k)
            if lw is not None:
                deps.add(lw)
        for k in w:
            lw = self.lastw.get(k)
            if lw is not None:
                deps.add(lw)
            for rd in self.readers.get(k, ()):
                deps.add(rd)
        for a in after:
            if a is not None:
                deps.add(a)
        for k in r:
            self.readers.setdefault(k, []).append(op)
        for k in w:
            self.lastw[k] = op
            self.readers[k] = []
        if dma:
            n = self.ndma[eng]
            self.ndma[eng] = n + 1
            op.sem_i = n % NSEM_DMA
            op.sem_k = n // NSEM_DMA + 1
            prev = self.dma_by_sem.get((eng, op.sem_i))
            if prev is not None:
                deps.add(prev)
            self.dma_by_sem[(eng, op.sem_i)] = op
        op.idx = len(self.ops[eng])
        self.ops[eng].append(op)
        return op

    def barrier(self):
        lasts = []
        for e in self.ENGS:
            if self.ops[e]:
                lasts.append(self.ops[e][-1])
        dmas = list(self.dma_by_sem.values())
        for e in self.ENGS:
            self.add(e, None, after=lasts + dmas, name="barrier")

    def finalize(self):
        for e in self.ENGS:
            for op in self.ops[e]:
                for d in op.deps:
                    if d.dma:
                        continue
                    if d.eng == "pe" and e == "pe":
                        continue
                    d.needs_inc = True
        for e in self.ENGS:
            c = 0
            for op in self.ops[e]:
                if op.needs_inc and not op.dma:
                    c += 1
                    op.ord = c

    def emit(self, e, engobj, sems, dsems):
        seen = {x: 0 for x in self.ENGS}
        seen_dma = {}
        for op in self.ops[e]:
            for d in sorted(op.deps, key=lambda o: (o.eng, o.idx)):
                if d.dma:
                    key = (d.eng, d.sem_i)
                    if seen_dma.get(key, 0) >= d.sem_k:
                        continue
                    engobj.wait_ge(dsems[d.eng][d.sem_i], 16 * d.sem_k)
                    seen_dma[key] = d.sem_k
                else:
                    if d.eng == "pe" and e == "pe":
                        continue
                    if seen[d.eng] >= d.ord:
                        continue
                    engobj.wait_ge(sems[d.eng], d.ord)
                    seen[d.eng] = d.ord
            if op.fn is None:
                if op.needs_inc:
                    engobj.sem_inc(sems[e], 1)
                continue
            ins = op.fn(engobj)
            if op.dma:
                ins.then_inc(dsems[e][op.sem_i], 16)
            elif op.needs_inc:
                ins.then_inc(sems[e], 1)


def _consts():
    ident = np.eye(128, dtype=np.float32)
    k = np.arange(128)[:, None]
    q = np.arange(128)[None, :]
    le = (k <= q).astype(np.float32)
    ge = (k >= q).astype(np.float32)
    z = np.zeros((128, 128), np.float32)
    masks = np.stack([np.concatenate([ge, le, ge, le], 1),
                      np.concatenate([z, le, ge, le], 1),
                      np.concatenate([z, le, z, le], 1)], 1)
    inv = 1.0 / (10000.0 ** (np.arange(0, 64, 2, dtype=np.float32) / np.float32(64)))
    t = np.arange(S, dtype=np.float32)
    ang = (t[:, None] * inv[None, :].astype(np.float32)).astype(np.float32)
    c = np.cos(ang).astype(np.float32)
    s = np.sin(ang).astype(np.float32)
    cs = np.concatenate([c, c], 1).reshape(NT, 128, 64).transpose(1, 0, 2)
    ss = np.concatenate([-s, s], 1).reshape(NT, 128, 64).transpose(1, 0, 2)
    sw = np.zeros((128, 128), np.float32)
    for m_ in range(128):
        sw[(m_ + 64) % 128, m_] = 1.0
    return dict(c_sw=sw, c_ident=ident, c_masks=np.ascontiguousarray(masks),
                c_negi=(-30000.0 * ident).astype(np.float32), c_nmasks=np.ascontiguousarray(1.0 - masks),
                c_cs=np.ascontiguousarray(cs), c_ss=np.ascontiguousarray(ss))


class StopBuild(Exception):
    pass


def build(layers=(0, 1, 2, 3), stop=None):
    nc = bass.Bass("TRN2", target_bir_lowering=False)

    def ck(name):
        if stop == name:
            raise StopBuild()
    sch = Sched()
    es = ExitStack()

    def din(name, shape):
        return nc.dram_tensor(name, list(shape), F32, kind="ExternalInput").ap()

    x_in = din("x", [S, D])
    a_norm = din("a_norm", [2, D])
    a_w_qkv = din("a_w_qkv", [2, D, 3 * D])
    a_q_gain = din("a_q_gain", [2, 64])
    a_k_gain = din("a_k_gain", [2, 64])
    a_lam = [din(n, [2, 64]) for n in ("a_lam_q1", "a_lam_k1", "a_lam_q2", "a_lam_k2")]
    a_sub_gain = din("a_sub_gain", [2, 128])
    a_w_o = din("a_w_o", [2, D, D])
    kv_norm = din("kv_norm", [D])
    kv_w = din("kv_w", [D, 6 * D])
    kv_k_gain = din("kv_k_gain", [3, 64])
    b_norm = din("b_norm", [2, D])
    b_w_q = din("b_w_q", [2, D, 3 * D])
    b_q_gain = din("b_q_gain", [2, 3, 64])
    b_w_o = din("b_w_o", [2, D, D])
    m_norm = din("m_norm", [4, D])
    m_w_up = din("m_w_up", [4, D, DFF])
    m_w_down = din("m_w_down", [4, DFF, D])
    c_ident = din("c_ident", [128, 128])
    c_sw = din("c_sw", [128, 128])
    c_negi = din("c_negi", [128, 128])
    c_nmasks = din("c_nmasks", [128, 3, 512])
    c_masks = din("c_masks", [128, 3, 512])
    c_cs = din("c_cs", [128, NT, 64])
    c_ss = din("c_ss", [128, NT, 64])
    y_out = nc.dram_tensor("y", [S, D], F32, kind="ExternalOutput").ap()
    kscr = nc.dram_tensor("kscr", [3, 8, 128, S], BF16, kind="Internal").ap()
    vscr = nc.dram_tensor("vscr", [3, 2, 128, NT, 512], BF16, kind="Internal").ap()

    def sb(name, shape, dt):
        return es.enter_context(nc.sbuf_tensor(name, list(shape), dt))

    def pst(name, shape, dt):
        return es.enter_context(nc.psum_tensor(name, list(shape), dt))

    xres = sb("xres", [128, NT, D], F32)
    hT = sb("hT", [128, NCH, S], BF16)
    ident = sb("ident", [128, 128], BF16)
    masks = sb("masks", [128, 3, 512], BF16)
    negi = sb("negi", [128, 128], BF16)
    cs_t = sb("cs_t", [128, NT, 64], F32)
    ss_t = sb("ss_t", [128, NT, 64], F32)
    gbc = sb("gbc", [128, D], F32)
    qkg = sb("qkg", [128, 512], F32)
    sgb = sb("sgb", [128, 128], F32)
    lamt = sb("lamt", [128, 4, 64], F32)
    lamv = sb("lamv", [128, 8], F32)
    swf = sb("swf", [128, 128], F32)
    gst = sb("gst", [128, 192], F32)
    hn = [sb(f"hn{i}", [128, D], BF16) for i in range(2)]
    ssq = [sb(f"ssq{i}", [128, 8], F32) for i in range(4)]
    REG = 88576
    regS = sb("regS", [128, REG], mybir.dt.uint8)

    class Carver:
        def __init__(self):
            self.off = 0

        def take(self, shape, dt):
            n = 1
            for s_ in shape[1:]:
                n *= s_
            nbytes = n * (4 if dt == F32 else 2)
            assert self.off + nbytes <= REG, (self.off, nbytes)
            v = regS[:, self.off:self.off + nbytes].bitcast(dt)
            self.off += nbytes
            if len(shape) == 3:
                v = v.rearrange("p (a b) -> p a b", b=shape[2])
            elif len(shape) == 4:
                v = v.rearrange("p (a b c) -> p a b c", b=shape[2], c=shape[3])
            return v

    psP = [pst(f"psP{i}", [128, 512], F32) for i in range(2)]
    psT = pst("psT", [128, 1024], BF16)
    psS = [pst(f"psS{i}", [128, 512], F32) for i in range(2)]
    psA = [pst(f"psA{i}", [128, 512], F32) for i in range(3)]

    psTb = psA[2][:].bitcast(BF16)
    TB = (psT, psTb)
    TK = ("psT", ("psA", 2))
    psTc = psP[1][:].bitcast(BF16)
    TB2 = (psT, psTc)
    SB3 = ((psS[0], ("psS", 0)), (psS[1], ("psS", 1)), (psP[1], ("psP", 1)))
    TK2 = ("psT", ("psP", 1))

    sems = {}
    dsems = {}
    for e in Sched.ENGS:
        sems[e] = es.enter_context(nc.semaphore(f"s_{e}"))
    for e in ("sp", "pool"):
        dsems[e] = [es.enter_context(nc.semaphore(f"d_{e}{i}")) for i in range(NSEM_DMA)]

    A = sch.add

    def C(name, *args, **kw):
        return lambda e: getattr(e, name)(*args, **kw)
    cnt = [0]

    def uid():
        cnt[0] += 1
        return cnt[0]

    def dma_sp(out, in_, r=(), w=(), after=()):
        return A("sp", C("dma_start", out=out, in_=in_), r=r, w=w, after=after, dma=True)

    def dma_pool(out, in_, r=(), w=(), after=()):
        return A("pool", C("dma_start", out=out, in_=in_), r=r, w=w, after=after, dma=True)

    dma_pool(ident[:], c_ident, w=["ident"])
    dma_pool(masks[:], c_nmasks, w=["masks"])
    dma_pool(negi[:], c_negi, w=["negi"])
    dma_sp(cs_t[:], c_cs, w=["cs"])
    dma_sp(swf[:], c_sw, w=["swf"])
    dma_sp(ss_t[:], c_ss, w=["ss"])
    xv = x_in.rearrange("(t p) d -> p t d", p=128)
    for g4 in range(4):
        dma_sp(xres[:, 4 * g4:4 * g4 + 4, :], xv[:, 4 * g4:4 * g4 + 4, :], w=[("x", t) for t in range(4 * g4, 4 * g4 + 4)])

    def rsqrt(out_ap, in_ap, c, keys):
        A("act", C("activation", out=out_ap, in_=in_ap, func=AF.Ln, bias=float(c)), r=keys, w=keys)
        A("act", C("activation", out=out_ap, in_=out_ap, func=AF.Exp, scale=-0.5), r=keys, w=keys)

    def load_bcast(dst_ap, src_row_ap, n, key, scale=None):
        dma_sp(dst_ap, src_row_ap.partition_broadcast(128), w=[key])
        if scale is not None:
            A("dve", C("tensor_scalar", out=dst_ap, in0=dst_ap, scalar1=float(scale), scalar2=None, op0=ALU.mult),
              r=[key], w=[key])

    def rmsnorm_to_hT(gain_row_ap):
        load_bcast(gbc[:], gain_row_ap, D, "gbc", scale=math.sqrt(D))
        for t in range(NT):
            sq = ssq[t % 4]
            hb = hn[t % 2]
            kx = ("x", t)
            A("act", C("activation", out=hb[:], in_=xres[:, t, :], func=AF.Square, accum_out=sq[:, 0:1]),
              r=[kx], w=[("ssq", t % 4), ("hn", t % 2)])
            rsqrt(sq[:, 1:2], sq[:, 0:1], D * EPS, [("ssq", t % 4)])
            A("dve", C("scalar_tensor_tensor", out=hb[:], in0=xres[:, t, :], scalar=sq[:, 1:2], in1=gbc[:],
                                                                        op0=ALU.mult, op1=ALU.mult),
              r=[kx, ("ssq", t % 4), "gbc"], w=[("hn", t % 2)])
            tb, tk = TB2[t % 2], TK2[t % 2]
            for c in range(NCH):
                A("pe", C("transpose", tb[:, c * 128:(c + 1) * 128], hb[:, c * 128:(c + 1) * 128], ident[:]),
                  r=[("hn", t % 2), "ident"], w=[tk])
            src = tb[:, :].rearrange("p (c k) -> p c k", k=128)
            eng = "act" if t % 2 == 0 else "dve"
            if eng == "act":
                A("act", C("copy", out=hT[:, :, t * 128:(t + 1) * 128], in_=src), r=[tk], w=[("hT", t)])
            else:
                A("dve", C("tensor_copy", out=hT[:, :, t * 128:(t + 1) * 128], in_=src), r=[tk], w=[("hT", t)])

    def qk_norm_rope_g(ps_ap, ncomp, t, out_bf, keys_r, key_w, tmp):
        n = ncomp * 64
        sqb, ya, yb, rs = tmp["sq"], tmp["ya"], tmp["yb"], tmp["rs"]
        kk = tmp["key"]
        A("act", C("activation", out=sqb[:, 0:n], in_=ps_ap, func=AF.Square), r=keys_r, w=[(kk, "sq")])
        yield
        A("dve", C("tensor_reduce", out=rs[:, 0:ncomp], in_=sqb[:, 0:n].rearrange("p (c d) -> p c d", d=64),
                   axis=AX.X, op=ALU.add), r=[(kk, "sq")], w=[(kk, "rs")])
        yield
        rsqrt(rs[:, 8:8 + ncomp], rs[:, 0:ncomp], 64 * EPS, [(kk, "rs")])
        yield
        ps3 = ps_ap.rearrange("p (c d) -> p c d", d=64)
        ya3 = ya[:, 0:n].rearrange("p (c d) -> p c d", d=64)
        yb3 = yb[:, 0:n].rearrange("p (c d) -> p c d", d=64)
        rsb = rs[:, 8:8 + ncomp].unsqueeze(2).to_broadcast([128, ncomp, 64])
        A("dve", C("tensor_tensor", out=ya3, in0=ps3, in1=rsb, op=ALU.mult), r=keys_r + [(kk, "rs")], w=[(kk, "ya")])
        A("dve", C("tensor_tensor", out=ya[:, 0:n], in0=ya[:, 0:n], in1=qkg[:, 0:n], op=ALU.mult),
          r=[(kk, "ya"), "qkg"], w=[(kk, "ya")])
        csb = cs_t[:, t, :].unsqueeze(1).to_broadcast([128, ncomp, 64])
        s1b = ss_t[:, t, 0:32].unsqueeze(1).to_broadcast([128, ncomp, 32])
        s2b = ss_t[:, t, 32:64].unsqueeze(1).to_broadcast([128, ncomp, 32])
        A("dve", C("tensor_tensor", out=yb3[:, :, 0:32], in0=ya3[:, :, 32:64], in1=s1b, op=ALU.mult),
          r=[(kk, "ya"), "ss"], w=[(kk, "yb")])
        A("dve", C("tensor_tensor", out=yb3[:, :, 32:64], in0=ya3[:, :, 0:32], in1=s2b, op=ALU.mult),
          r=[(kk, "ya"), "ss"], w=[(kk, "yb")])
        A("dve", C("tensor_tensor", out=ya3, in0=ya3, in1=csb, op=ALU.mult), r=[(kk, "ya"), "cs"], w=[(kk, "ya")])
        A("dve", C("tensor_tensor", out=out_bf, in0=ya[:, 0:n], in1=yb[:, 0:n], op=ALU.add),
          r=[(kk, "ya"), (kk, "yb")], w=[key_w])
        yield

    def qk_norm_rope(ps_ap, ncomp, t, out_bf, keys_r, key_w, tmp):
        n = ncomp * 64
        sqb, ya, yb, rs = tmp["sq"], tmp["ya"], tmp["yb"], tmp["rs"]
        kk = tmp["key"]
        A("act", C("activation", out=sqb[:, 0:n], in_=ps_ap, func=AF.Square), r=keys_r, w=[(kk, "sq")])
        A("dve", C("tensor_reduce", out=rs[:, 0:ncomp], in_=sqb[:, 0:n].rearrange("p (c d) -> p c d", d=64),
                                           axis=AX.X, op=ALU.add), r=[(kk, "sq")], w=[(kk, "rs")])
        rsqrt(rs[:, 8:8 + ncomp], rs[:, 0:ncomp], 64 * EPS, [(kk, "rs")])
        ps3 = ps_ap.rearrange("p (c d) -> p c d", d=64)
        ya3 = ya[:, 0:n].rearrange("p (c d) -> p c d", d=64)
        yb3 = yb[:, 0:n].rearrange("p (c d) -> p c d", d=64)
        rsb = rs[:, 8:8 + ncomp].unsqueeze(2).to_broadcast([128, ncomp, 64])
        A("dve", C("tensor_tensor", out=ya3, in0=ps3, in1=rsb, op=ALU.mult), r=keys_r + [(kk, "rs")], w=[(kk, "ya")])
        A("dve", C("tensor_tensor", out=ya[:, 0:n], in0=ya[:, 0:n], in1=qkg[:, 0:n], op=ALU.mult),
          r=[(kk, "ya"), "qkg"], w=[(kk, "ya")])
        csb = cs_t[:, t, :].unsqueeze(1).to_broadcast([128, ncomp, 64])
        s1b = ss_t[:, t, 0:32].unsqueeze(1).to_broadcast([128, ncomp, 32])
        s2b = ss_t[:, t, 32:64].unsqueeze(1).to_broadcast([128, ncomp, 32])
        A("dve", C("tensor_tensor", out=yb3[:, :, 0:32], in0=ya3[:, :, 32:64], in1=s1b, op=ALU.mult),
          r=[(kk, "ya"), "ss"], w=[(kk, "yb")])
        A("dve", C("tensor_tensor", out=yb3[:, :, 32:64], in0=ya3[:, :, 0:32], in1=s2b, op=ALU.mult),
          r=[(kk, "ya"), "ss"], w=[(kk, "yb")])
        A("dve", C("tensor_tensor", out=ya3, in0=ya3, in1=csb, op=ALU.mult), r=[(kk, "ya"), "cs"], w=[(kk, "ya")])
        A("dve", C("tensor_tensor", out=out_bf, in0=ya[:, 0:n], in1=yb[:, 0:n], op=ALU.add),
          r=[(kk, "ya"), (kk, "yb")], w=[key_w])

    def out_proj_add(oT_ap, okeys, wo_ap, wkey):
        i = 0
        for t in range(NT):
            for h in range(2):
                pb = psA[i % 3]
                pk = ("psA", i % 3)
                i += 1
                A("pe", C("matmul", pb[:], oT_ap[:, t * 128:(t + 1) * 128], wo_ap[:, h * 512:(h + 1) * 512],
                                                            start=True, stop=True),
                  r=okeys(t) + [wkey], w=[pk])
                A("dve", C("tensor_tensor", out=xres[:, t, h * 512:(h + 1) * 512], in0=pb[:],
                                                                    in1=xres[:, t, h * 512:(h + 1) * 512], op=ALU.add),
                  r=[pk, ("x", t)], w=[("x", t)])

    def layer_a(l):
        cv = Carver()
        wsl = [cv.take([128, NCH, 384], BF16) for _ in range(2)]
        wos = [cv.take([128, D], BF16) for _ in range(2)]
        qT = [cv.take([128, S], BF16) for _ in range(2)]
        kT = [cv.take([128, S], BF16) for _ in range(2)]
        vh = [cv.take([128, NT, 130], BF16) for _ in range(2)]
        oT = [cv.take([128, S], BF16) for _ in range(2)]
        qkb = [cv.take([128, 256], BF16) for _ in range(2)]
        Et = [cv.take([128, 512], BF16) for _ in range(4)]
        tmp = dict(sq=cv.take([128, 512], F32), ya=cv.take([128, 512], F32), yb=cv.take([128, 512], F32),
                   rs=cv.take([128, 16], F32), key="tmpA")
        o1 = [cv.take([128, 128], F32) for _ in range(2)]
        o2 = [cv.take([128, 128], F32) for _ in range(2)]
        onb = [cv.take([128, 128], BF16) for _ in range(2)]
        sm = [cv.take([128, 8], F32) for _ in range(2)]
        lam_init = 0.8 - 0.6 * math.exp(-0.3 * l)

        rmsnorm_to_hT(a_norm[l])
        ck("norm")
        for j in range(2):
            dma_sp(qkg[:, j * 64:(j + 1) * 64], a_q_gain[l].partition_broadcast(128), r=["qkg"], w=[("qkgp", j)])
            dma_sp(qkg[:, 128 + j * 64:128 + (j + 1) * 64], a_k_gain[l].partition_broadcast(128), r=["qkg"], w=[("qkgp", 2 + j)])
        A("dve", C("tensor_scalar", out=qkg[:, 0:256], in0=qkg[:, 0:256], scalar1=8.0, scalar2=None, op0=ALU.mult),
          r=[("qkgp", j) for j in range(4)], w=["qkg"] + [("qkgp", j) for j in range(4)])
        ck("g1")
        load_bcast(sgb[:], a_sub_gain[l], 128, "sgb", scale=math.sqrt(128.0) * (1.0 - lam_init))
        ck("g2")
        for j in range(4):
            dma_sp(lamt[:, j, :], a_lam[j][l].partition_broadcast(128), r=["lamt"], w=[("lamtp", j)])
        A("dve", C("tensor_tensor", out=lamt[:, 0, :], in0=lamt[:, 0, :], in1=lamt[:, 1, :], op=ALU.mult),
          r=[("lamtp", j) for j in range(4)], w=["lamt"] + [("lamtp", j) for j in range(4)])
        A("dve", C("tensor_tensor", out=lamt[:, 2, :], in0=lamt[:, 2, :], in1=lamt[:, 3, :], op=ALU.mult), r=["lamt"], w=["lamt"])
        ck("g3")
        A("dve", C("tensor_reduce", out=lamv[:, 0:1], in_=lamt[:, 0, :], axis=AX.X, op=ALU.add), r=["lamt"], w=["lamv"])
        A("dve", C("tensor_reduce", out=lamv[:, 1:2], in_=lamt[:, 2, :], axis=AX.X, op=ALU.add), r=["lamt"], w=["lamv"])
        ck("g4")
        A("act", C("activation", out=lamv[:, 2:4], in_=lamv[:, 0:2], func=AF.Exp), r=["lamv"], w=["lamv"])
        ck("g5")
        A("dve", C("tensor_scalar", out=lamv[:, 5:6], in0=lamv[:, 2:3], scalar1=-1.0, scalar2=float(-lam_init),
                   op0=ALU.mult, op1=ALU.add), r=["lamv"], w=["lamv"])
        A("dve", C("tensor_tensor", out=lamv[:, 4:5], in0=lamv[:, 5:6], in1=lamv[:, 3:4], op=ALU.add), r=["lamv"], w=["lamv"])
        wq = a_w_qkv[l].rearrange("(c p) n -> p c n", p=128)
        ck("gains")
        for hd in range(8):
            b = hd % 2
            kws = [("wsl", b, j) for j in range(3)]
            for j in range(3):
                dma_pool(wsl[b][:, :, j * 128:(j + 1) * 128], wq[:, :, j * D + hd * 128: j * D + (hd + 1) * 128], w=[kws[j]])
            dma_pool(wos[b][:], a_w_o[l][hd * 128:(hd + 1) * 128, :], w=[("wos", b)])
            if hd == 0:
                for bb in range(2):
                    A("dve", C("memset", vh[bb][:, :, 128:130], 1.0), w=[("vh", bb, c4) for c4 in range(NT)])
            ck("p0")
            for t in range(NT):
                pb = psP[t % 2]
                pk = ("psP", t % 2)
                for c in range(NCH):
                    A("pe", C("matmul", pb[:, 0:384], hT[:, c, t * 128:(t + 1) * 128], wsl[b][:, c, :],
                                                                start=(c == 0), stop=(c == NCH - 1)),
                      r=[("hT", t)] + kws, w=[pk])
                ck("p1")
                A("act", C("copy", out=vh[b][:, t, 0:128], in_=pb[:, 256:384]), r=[pk], w=[("vh", b, t)])
                ck("p2")
                qb_ = qkb[t % 2]
                qk_norm_rope(pb[:, 0:256], 4, t, qb_[:], [pk], ("qkb", t % 2), tmp)
                ck("p3")
                j4 = t % 4
                A("pe", C("transpose", psT[:, j4 * 128:(j4 + 1) * 128], qb_[:, 0:128], ident[:]),
                  r=[("qkb", t % 2), "ident"], w=["psT"])
                A("pe", C("transpose", psTb[:, j4 * 128:(j4 + 1) * 128], qb_[:, 128:256], ident[:]),
                  r=[("qkb", t % 2), "ident"], w=[TK[1]])
                ck(f"p4_{t}")
                if j4 == 3:
                    c4 = t // 4
                    A("act", C("copy", out=qT[b][:, c4 * 512:(c4 + 1) * 512], in_=psT[:, 0:512]), r=["psT"], w=[("qT", b, c4)])
                    A("dve", C("tensor_copy", out=kT[b][:, c4 * 512:(c4 + 1) * 512], in_=psTb[:, 0:512]), r=[TK[1]], w=[("kT", b, c4)])
                ck(f"p5_{t}")
            ck("proj")
            ei = 0
            si = 0
            for j in range(4):
                def acc(c, i):
                    if i < 3:
                        return psA[c][:, i * 129:(i + 1) * 129], ("psA", c)
                    return psA[2][:, c * 129:(c + 1) * 129], ("psA", 2)
                for kb in range(4 * j + 4):
                    nq0 = max(kb, 4 * j)
                    N = (4 * j + 4 - nq0) * 128
                    for c in range(2):
                        sbk = psS[si % 2]
                        sk = ("psS", si % 2)
                        si += 1
                        E = Et[ei % 4]
                        ek = ("E", ei % 4)
                        ei += 1
                        A("pe", C("matmul", sbk[:, 0:N], kT[b][64 * c:64 * c + 64, kb * 128:(kb + 1) * 128],
                            qT[b][64 * c:64 * c + 64, nq0 * 128:nq0 * 128 + N], start=True, stop=True),
                          r=[("kT", b, kb // 4), ("qT", b, j)], w=[sk])
                        A("act", C("activation", out=E[:, 0:N], in_=sbk[:, 0:N], func=AF.Exp, scale=0.125),
                          r=[sk], w=[ek])
                        if kb >= 4 * j:
                            A("dve", C("tensor_tensor", out=E[:, 0:128], in0=E[:, 0:128], in1=masks[:, 0, 128:256], op=ALU.mult),
                              r=[ek, "masks"], w=[ek])
                        for qb in range(nq0, 4 * j + 4):
                            ap_, ak = acc(c, qb - 4 * j)
                            A("pe", C("matmul", ap_, E[:, (qb - nq0) * 128:(qb - nq0 + 1) * 128], vh[b][:, kb, 0:129],
                                start=(kb == 0 and ((qb - 4 * j) == 0 or ((qb - 4 * j) == 3 and c == 0))), stop=(kb == qb), skip_group_check=True),
                              r=[ek, ("vh", b, kb)], w=[ak])
                for i in range(4):
                    qb = 4 * j + i
                    a0, k0 = acc(0, i)
                    a1, k1 = acc(1, i)
                    s_ = sm[i % 2]
                    sk_ = ("sm", i % 2)
                    o1_, o2_, on_ = o1[i % 2], o2[i % 2], onb[i % 2]
                    A("dve", C("reciprocal", out=s_[:, 0:1], in_=a0[:, 128:129]), r=[k0], w=[sk_])
                    A("dve", C("reciprocal", out=s_[:, 1:2], in_=a1[:, 128:129]), r=[k1], w=[sk_])
                    A("dve", C("tensor_tensor", out=s_[:, 1:2], in0=s_[:, 1:2], in1=lamv[:, 4:5], op=ALU.mult),
                      r=[sk_, "lamv"], w=[sk_])
                    A("act", C("activation", out=o1_[:], in_=a0[:, 0:128], func=AF.Copy, scale=s_[:, 0:1]),
                      r=[k0, sk_], w=[("o1", i % 2)])
                    A("dve", C("scalar_tensor_tensor", out=o2_[:], in0=a1[:, 0:128], scalar=s_[:, 1:2], in1=o1_[:], op0=ALU.mult, op1=ALU.add),
                      r=[k1, sk_, ("o1", i % 2)], w=[("o2", i % 2)])
                    A("act", C("activation", out=on_[:], in_=o2_[:], func=AF.Square, accum_out=s_[:, 2:3]), r=[("o2", i % 2)], w=[sk_, ("onb", i % 2)])
                    rsqrt(s_[:, 3:4], s_[:, 2:3], 128 * EPS, [sk_])
                    A("dve", C("scalar_tensor_tensor", out=on_[:], in0=o2_[:], scalar=s_[:, 3:4], in1=sgb[:], op0=ALU.mult, op1=ALU.mult),
                      r=[("o2", i % 2), sk_, "sgb"], w=[("onb", i % 2)])
                    A("pe", C("transpose", psT[:, i * 128:(i + 1) * 128], on_[:], ident[:]),
                      r=[("onb", i % 2), "ident"], w=["psT"])
                A("act", C("copy", out=oT[b][:, j * 512:(j + 1) * 512], in_=psT[:, 0:512]), r=["psT"], w=[("oT", b, j)])
            ck("attn")
            out_proj_add(oT[b], lambda t: [("oT", b, t // 4)], wos[b], ("wos", b))
            ck("oproj")

    def strided_tok(ap2d, r, c, n):
        return ap2d.rearrange("p (n b r) -> p n r b", b=128, r=r)[:, n, c, :]

    def kv_phase():
        sch.barrier()
        rmsnorm_to_hT(kv_norm)
        wk = kv_w.rearrange("(c p) n -> p c n", p=128)
        cv = Carver()
        wkv = [cv.take([128, NCH, 512], BF16) for _ in range(2)]
        kst = [cv.take([128, 4, S], BF16) for _ in range(2)]
        kb16 = [cv.take([128, 512], BF16) for _ in range(2)]
        tmp = dict(sq=cv.take([128, 512], F32), ya=cv.take([128, 512], F32), yb=cv.take([128, 512], F32),
                   rs=cv.take([128, 16], F32), key="tmpK")
        for cg in range(6):
            g, half = cg // 2, cg % 2
            b = cg % 2
            dma_pool(wkv[b][:], wk[:, :, cg * 512:(cg + 1) * 512], w=[("wkv", b)])
            if half == 0:
                dma_sp(gst[:, 0:64], kv_k_gain[g].partition_broadcast(128), r=["qkg"], w=["gst"])
                A("dve", C("tensor_scalar", out=qkg[:, 0:512].rearrange("p (h d) -> p h d", d=64),
                           in0=gst[:, 0:64].unsqueeze(1).to_broadcast([128, 8, 64]), scalar1=8.0, scalar2=None, op0=ALU.mult),
                  r=["gst"], w=["qkg"])
            for t in range(NT + 1):
                if t < NT:
                    pb = psP[t % 2]
                    pk = ("psP", t % 2)
                    for c in range(NCH):
                        A("pe", C("matmul", pb[:], hT[:, c, t * 128:(t + 1) * 128], wkv[b][:, c, :], start=(c == 0), stop=(c == NCH - 1)),
                          r=[("hT", t), ("wkv", b)], w=[pk])
                    qk_norm_rope(pb[:, 0:512], 8, t, kb16[t % 2][:], [pk], ("kb16", t % 2), tmp)
                if t >= 1:
                    tp = t - 1
                    kb_ = kb16[tp % 2]
                    hf = tp % 2
                    for pr in range(4):
                        A("pe", C("transpose", TB[hf][:, pr * 128:(pr + 1) * 128], kb_[:, pr * 128:(pr + 1) * 128], ident[:]),
                          r=[("kb16", tp % 2), "ident"], w=[TK[hf]])
                    src = TB[hf][:, 0:512].rearrange("p (a k) -> p a k", k=128)
                    if tp % 2 == 0:
                        A("act", C("copy", out=kst[b][:, :, tp * 128:(tp + 1) * 128], in_=src), r=[TK[hf]], w=[("kst", b, tp)])
                    else:
                        A("dve", C("tensor_copy", out=kst[b][:, :, tp * 128:(tp + 1) * 128], in_=src), r=[TK[hf]], w=[("kst", b, tp)])
            dma_sp(kscr[g, 4 * half:4 * half + 4].rearrange("a p s -> p a s"), kst[b][:, :, :],
                   r=[("kst", b, t) for t in range(NT)], w=[("kscr", g, 4 * half + pr) for pr in range(4)])
        sch.barrier()
        cv = Carver()
        wkv = [cv.take([128, NCH, 512], BF16) for _ in range(2)]
        vst = [cv.take([128, NT, 512], BF16) for _ in range(2)]
        allh = [("hT", t) for t in range(NT)]
        for cg in range(6):
            g, half = cg // 2, cg % 2
            r_ = GROUPS[g][1]
            nb = 16 // r_
            b = cg % 2
            dma_pool(wkv[b][:], wk[:, :, 3 * D + cg * 512: 3 * D + (cg + 1) * 512], w=[("wkv", b)])
            for pt in range(NT):
                c_, n_ = pt // nb, pt % nb
                pb = psP[pt % 2]
                pk = ("psP", pt % 2)
                for c in range(NCH):
                    A("pe", C("matmul", pb[:], strided_tok(hT[:, c, :], r_, c_, n_), wkv[b][:, c, :], start=(c == 0), stop=(c == NCH - 1)),
                      r=allh + [("wkv", b)], w=[pk])
                if pt % 2 == 0:
                    A("act", C("copy", out=vst[b][:, pt, :], in_=pb[:]), r=[pk], w=[("vst", b, pt)])
                else:
                    A("dve", C("tensor_copy", out=vst[b][:, pt, :], in_=pb[:]), r=[pk], w=[("vst", b, pt)])
            dma_sp(vscr[g, half], vst[b][:, :, :], r=[("vst", b, pt) for pt in range(NT)], w=[("vscr", g, half)])

    def layer_b(bl):
        cv = Carver()
        wsl = [cv.take([128, NCH, 384], BF16) for _ in range(2)]
        wos = cv.take([128, D], BF16)
        qT3 = cv.take([128, 3, S], BF16)
        kTg = [cv.take([128, S], BF16) for _ in range(2)]
        vaug = [cv.take([128, NT, 192], BF16) for _ in range(2)]
        accS = [cv.take([128, S], F32) for _ in range(2)]
        oT = cv.take([128, S], BF16)
        rcp = cv.take([128, 512], F32)
        qkb = [cv.take([128, 384], BF16) for _ in range(2)]
        Et = [cv.take([128, 512], BF16) for _ in range(3)]
        tmp = dict(sq=cv.take([128, 384], F32), ya=cv.take([128, 384], F32), yb=cv.take([128, 384], F32),
                   rs=cv.take([128, 16], F32), key="tmpB")
        rmsnorm_to_hT(b_norm[bl])
        dma_sp(gst[:, 0:192], b_q_gain[bl].rearrange("g d -> (g d)").partition_broadcast(128), r=["qkg"], w=["gst"])
        A("dve", C("tensor_scalar", out=qkg[:, 0:384].rearrange("p (g a d) -> p g a d", a=2, d=64),
                   in0=gst[:, 0:192].rearrange("p (g d) -> p g d", d=64).unsqueeze(2).to_broadcast([128, 3, 2, 64]),
                   scalar1=8.0, scalar2=None, op0=ALU.mult), r=["gst"], w=["qkg"])
        for bb in range(2):
            A("dve", C("memset", vaug[bb][:, :, 64:128], 1.0), w=[("vaug", bb, 0), ("vaug", bb, 1), ("vaugo", bb)])
        wq = b_w_q[bl].rearrange("(c p) n -> p c n", p=128)
        ei = 0
        si = 0
        oi = 0
        li = 0
        for j in range(8):
            b = j % 2
            kws = [("wsl", b, g) for g in range(3)]
            for g in range(3):
                dma_pool(wsl[b][:, :, g * 128:(g + 1) * 128], wq[:, :, g * D + j * 128: g * D + (j + 1) * 128], w=[kws[g]])
            dma_pool(wos[:], b_w_o[bl][j * 128:(j + 1) * 128, :], w=["wos"])
            for t in range(NT):
                pb = psP[t % 2]
                pk = ("psP", t % 2)
                for c in range(NCH):
                    A("pe", C("matmul", pb[:, 0:384], hT[:, c, t * 128:(t + 1) * 128], wsl[b][:, c, :], start=(c == 0), stop=(c == NCH - 1)),
                      r=[("hT", t)] + kws, w=[pk])
                qb_ = qkb[t % 2]
                qk_norm_rope(pb[:, 0:384], 6, t, qb_[:], [pk], ("qkb", t % 2), tmp)
                hf = t % 2
                for g in range(3):
                    A("pe", C("transpose", TB[hf][:, g * 128:(g + 1) * 128], qb_[:, g * 128:(g + 1) * 128], ident[:]),
                      r=[("qkb", t % 2), "ident"], w=[TK[hf]])
                src = TB[hf][:, 0:384].rearrange("p (a k) -> p a k", k=128)
                if t % 2 == 0:
                    A("act", C("copy", out=qT3[:, :, t * 128:(t + 1) * 128], in_=src), r=[TK[hf]], w=[("qT3", t)])
                else:
                    A("dve", C("tensor_copy", out=qT3[:, :, t * 128:(t + 1) * 128], in_=src), r=[TK[hf]], w=[("qT3", t)])
            allq = [("qT3", t) for t in range(NT)]
            for g in range(3):
                r_ = GROUPS[g][1]
                nb = 16 // r_
                bb = li % 2
                li += 1
                half, hh = j // 4, 2 * (j % 4)
                dma_sp(kTg[bb][:], kscr[g, j], r=[("kscr", g, j)], w=[("kTg", bb)])
                dma_sp(vaug[bb][:, :, 0:64], vscr[g, half][:, :, hh * 64:(hh + 1) * 64], r=[("vscr", g, half)], w=[("vaug", bb, 0)])
                dma_sp(vaug[bb][:, :, 128:192], vscr[g, half][:, :, (hh + 1) * 64:(hh + 2) * 64], r=[("vscr", g, half)], w=[("vaug", bb, 1)])
                vkeys = [("vaug", bb, 0), ("vaug", bb, 1), ("vaugo", bb)]
                for X in range(2):
                    rows = slice(64 * X, 64 * X + 64)
                    vc = slice(64 * X, 64 * X + 128)
                    accv = accS[X].rearrange("p (n b r) -> p r n b", b=128, r=r_)
                    for quad in range(4):
                        ob = psA[oi % 3]
                        ok = ("psA", oi % 3)
                        oi += 1
                        for h2 in range(2):
                            blks = (4 * quad + 2 * h2, 4 * quad + 2 * h2 + 1)
                            sbk = psS[si % 2]
                            sk = ("psS", si % 2)
                            si += 1
                            E = Et[ei % 3]
                            ek = ("E", ei % 3)
                            ei += 1
                            for ub, blk in enumerate(blks):
                                c_, n_ = blk // nb, blk % nb
                                qap = strided_tok(qT3[rows, g, :], r_, c_, n_)
                                for u2, kn in enumerate((n_ - 1 if n_ > 0 else n_, n_)):
                                    u = 2 * ub + u2
                                    A("pe", C("matmul", sbk[:, u * 128:(u + 1) * 128], strided_tok(kTg[bb][rows, :], r_, c_, kn), qap,
                                              start=True, stop=True),
                                      r=[("kTg", bb)] + allq, w=[sk])
                            A("act", C("activation", out=E[:], in_=sbk[:], func=AF.Exp, scale=0.125), r=[sk], w=[ek])
                            n0, n1 = blks[0] % nb, blks[1] % nb
                            mi = 0 if (n0 > 0 and n1 > 0) else (1 if n1 > 0 else 2)
                            A("dve", C("tensor_tensor", out=E[:], in0=E[:], in1=masks[:, mi, :], op=ALU.mult), r=[ek, "masks"], w=[ek])
                            for ub, blk in enumerate(blks):
                                n_ = blk % nb
                                slot = blk - 4 * quad
                                oap = ob[:, slot * 128:(slot + 1) * 128]
                                if n_ > 0:
                                    A("pe", C("matmul", oap, vaug[bb][:, blk - 1, vc], E[:, (2 * ub) * 128:(2 * ub + 1) * 128],
                                              start=True, stop=False), r=[ek] + vkeys, w=[ok])
                                A("pe", C("matmul", oap, vaug[bb][:, blk, vc], E[:, (2 * ub + 1) * 128:(2 * ub + 2) * 128],
                                          start=(n_ == 0), stop=True), r=[ek] + vkeys, w=[ok])
                        if g == 0:
                            dst = accv[:, 0, 4 * quad:4 * quad + 4, :]
                        elif g == 1:
                            dst = accv[:, quad, 0:4, :]
                        else:
                            dst = accv[:, 4 * quad:4 * quad + 4, 0, :]
                        srcv = ob[:].rearrange("p (a k) -> p a k", k=128)
                        if g == 0:
                            A("act", C("copy", out=dst, in_=srcv), r=[ok], w=[("acc", X)])
                        else:
                            A("dve", C("tensor_tensor", out=dst, in0=srcv, in1=dst, op=ALU.add), r=[ok, ("acc", X)], w=[("acc", X)])
            for X in range(2):
                rows = slice(64 * X, 64 * X + 64)
                for ck in range(4):
                    ob = psA[oi % 3]
                    ok = ("psA", oi % 3)
                    oi += 1
                    A("pe", C("matmul", ob[:], swf[:], accS[X][:, ck * 512:(ck + 1) * 512], start=True, stop=True),
                      r=[("acc", X), "swf"], w=[ok])
                    A("dve", C("reciprocal", out=rcp[rows, :], in_=ob[rows, :]), r=[ok], w=[("rcp", X)])
                    A("dve", C("tensor_tensor", out=oT[rows, ck * 512:(ck + 1) * 512], in0=accS[X][rows, ck * 512:(ck + 1) * 512],
                               in1=rcp[rows, :], op=ALU.mult), r=[("rcp", X), ("acc", X)], w=[("oTb", X, ck)])
            out_proj_add(oT, lambda t: [("oTb", 0, t // 4), ("oTb", 1, t // 4)], wos, "wos")

    def merge(ga, na, gb, nb):
        ia = ib = 0
        da = db = False
        while not (da and db):
            if not da and (db or ia * nb <= ib * na):
                try:
                    next(ga)
                except StopIteration:
                    da = True
                ia += 1
            else:
                try:
                    next(gb)
                except StopIteration:
                    db = True
                ib += 1

    def drain(g):
        for _ in g:
            pass

    def out_proj_gen(oT_ap, okeys, wo_ap, wkey):
        i = 0
        for t in range(NT):
            for h in range(2):
                pb = psA[i % 3]
                pk = ("psA", i % 3)
                i += 1
                A("pe", C("matmul", pb[:], oT_ap[:, t * 128:(t + 1) * 128], wo_ap[:, h * 512:(h + 1) * 512], start=True, stop=True),
                  r=okeys(t) + [wkey], w=[pk])
                A("dve", C("tensor_tensor", out=xres[:, t, h * 512:(h + 1) * 512], in0=pb[:], in1=xres[:, t, h * 512:(h + 1) * 512], op=ALU.add),
                  r=[pk, ("x", t)], w=[("x", t)])
            yield

    def mergeN(gens, weights):
        n = len(gens)
        prog = [0] * n
        done = [False] * n
        while not all(done):
            k = min((i for i in range(n) if not done[i]), key=lambda i: prog[i] / weights[i])
            try:
                next(gens[k])
            except StopIteration:
                done[k] = True
            prog[k] += 1

    def layer_a2(l):
        cv = Carver()
        wsl = [cv.take([128, NCH, 384], BF16) for _ in range(2)]
        wos = [cv.take([128, D], BF16) for _ in range(2)]
        qT = [cv.take([128, S], BF16) for _ in range(2)]
        kT = [cv.take([128, S], BF16) for _ in range(2)]
        vh = [cv.take([128, NT, 130], BF16) for _ in range(2)]
        oT = [cv.take([128, S], BF16) for _ in range(4)]
        qkb = [cv.take([128, 256], BF16) for _ in range(3)]
        yps = [cv.take([128, 384], F32) for _ in range(2)]
        Et = [cv.take([128, 512], BF16) for _ in range(4)]
        accc = [[cv.take([128, 387], F32) for _ in range(2)] for _ in range(2)]
        acc3 = [cv.take([128, 258], F32) for _ in range(2)]
        tmp = dict(sq=cv.take([128, 512], F32), ya=cv.take([128, 512], F32), yb=cv.take([128, 512], F32),
                   rs=cv.take([128, 16], F32), key="tmpA")
        o1 = [cv.take([128, 128], F32) for _ in range(2)]
        o2 = [cv.take([128, 128], F32) for _ in range(2)]
        onb = [cv.take([128, 128], BF16) for _ in range(2)]
        sm = [cv.take([128, 8], F32) for _ in range(2)]
        lam_init = 0.8 - 0.6 * math.exp(-0.3 * l)

        rmsnorm_to_hT(a_norm[l])
        for j in range(2):
            dma_sp(qkg[:, j * 64:(j + 1) * 64], a_q_gain[l].partition_broadcast(128), r=["qkg"], w=[("qkgp", j)])
            dma_sp(qkg[:, 128 + j * 64:128 + (j + 1) * 64], a_k_gain[l].partition_broadcast(128), r=["qkg"], w=[("qkgp", 2 + j)])
        A("dve", C("tensor_scalar", out=qkg[:, 0:256], in0=qkg[:, 0:256], scalar1=8.0, scalar2=None, op0=ALU.mult),
          r=[("qkgp", j) for j in range(4)], w=["qkg"] + [("qkgp", j) for j in range(4)])
        load_bcast(sgb[:], a_sub_gain[l], 128, "sgb", scale=math.sqrt(128.0) * (1.0 - lam_init))
        for j in range(4):
            dma_sp(lamt[:, j, :], a_lam[j][l].partition_broadcast(128), r=["lamt"], w=[("lamtp", j)])
        A("dve", C("tensor_tensor", out=lamt[:, 0, :], in0=lamt[:, 0, :], in1=lamt[:, 1, :], op=ALU.mult),
          r=[("lamtp", j) for j in range(4)], w=["lamt"] + [("lamtp", j) for j in range(4)])
        A("dve", C("tensor_tensor", out=lamt[:, 2, :], in0=lamt[:, 2, :], in1=lamt[:, 3, :], op=ALU.mult), r=["lamt"], w=["lamt"])
        A("dve", C("tensor_reduce", out=lamv[:, 0:1], in_=lamt[:, 0, :], axis=AX.X, op=ALU.add), r=["lamt"], w=["lamv"])
        A("dve", C("tensor_reduce", out=lamv[:, 1:2], in_=lamt[:, 2, :], axis=AX.X, op=ALU.add), r=["lamt"], w=["lamv"])
        A("act", C("activation", out=lamv[:, 2:4], in_=lamv[:, 0:2], func=AF.Exp), r=["lamv"], w=["lamv"])
        A("dve", C("tensor_scalar", out=lamv[:, 5:6], in0=lamv[:, 2:3], scalar1=-1.0, scalar2=float(-lam_init),
                   op0=ALU.mult, op1=ALU.add), r=["lamv"], w=["lamv"])
        A("dve", C("tensor_tensor", out=lamv[:, 4:5], in0=lamv[:, 5:6], in1=lamv[:, 3:4], op=ALU.add), r=["lamv"], w=["lamv"])
        for bb in range(2):
            A("dve", C("memset", vh[bb][:, :, 128:130], 1.0), w=[("vh", bb, c4) for c4 in range(NT)])
        wq = a_w_qkv[l].rearrange("(c p) n -> p c n", p=128)
        PP, PPK = psP[0], ("psP", 0)

        def proj_gen(hd):
            b = hd % 2
            kws = [("wsl", b, j) for j in range(3)]
            for j in range(3):
                dma_pool(wsl[b][:, :, j * 128:(j + 1) * 128], wq[:, :, j * D + hd * 128: j * D + (hd + 1) * 128], w=[kws[j]])
            for t in range(NT + 2):
                if t >= 2:
                    tp = t - 2
                    qb_ = qkb[tp % 3]
                    A("pe", C("transpose", psT[:, 0:128], qb_[:, 0:128], ident[:]), r=[("qkb", tp % 3), "ident"], w=["psT"])
                    A("pe", C("transpose", psT[:, 128:256], qb_[:, 128:256], ident[:]), r=[("qkb", tp % 3), "ident"], w=["psT"])
                    A("act", C("copy", out=qT[b][:, tp * 128:(tp + 1) * 128], in_=psT[:, 0:128]), r=["psT"], w=[("qT", b, tp // 4)])
                    A("dve", C("tensor_copy", out=kT[b][:, tp * 128:(tp + 1) * 128], in_=psT[:, 128:256]), r=["psT"], w=[("kT", b, tp // 4)])
                    yield
                if t < NT:
                    for c in range(NCH):
                        A("pe", C("matmul", PP[:, 0:384], hT[:, c, t * 128:(t + 1) * 128], wsl[b][:, c, :],
                                  start=(c == 0), stop=(c == NCH - 1)), r=[("hT", t)] + kws, w=[PPK])
                    y_ = yps[t % 2]
                    yk = ("yps", t % 2)
                    A("dve", C("tensor_copy", out=y_[:], in_=PP[:, 0:384]), r=[PPK], w=[yk])
                    A("dve", C("tensor_copy", out=vh[b][:, t, 0:128], in_=y_[:, 256:384]), r=[yk], w=[("vh", b, t)])
                    for _ in qk_norm_rope_g(y_[:, 0:256], 4, t, qkb[t % 3][:], [yk], ("qkb", t % 3), tmp):
                        yield

        def fin_gen(hd, j, buf):
            ob_ = hd % 4
            for i in range(4):
                if i < 3:
                    a0 = accc[buf][0][:, i * 129:(i + 1) * 129]
                    a1 = accc[buf][1][:, i * 129:(i + 1) * 129]
                    k0, k1 = ("accc", buf, 0), ("accc", buf, 1)
                else:
                    a0 = acc3[buf][:, 0:129]
                    a1 = acc3[buf][:, 129:258]
                    k0 = k1 = ("acc3", buf)
                s_ = sm[i % 2]
                sk_ = ("sm", i % 2)
                o1_, o2_, on_ = o1[i % 2], o2[i % 2], onb[i % 2]
                A("dve", C("reciprocal", out=s_[:, 0:1], in_=a0[:, 128:129]), r=[k0], w=[sk_])
                A("dve", C("reciprocal", out=s_[:, 1:2], in_=a1[:, 128:129]), r=[k1], w=[sk_])
                A("dve", C("tensor_tensor", out=s_[:, 1:2], in0=s_[:, 1:2], in1=lamv[:, 4:5], op=ALU.mult), r=[sk_, "lamv"], w=[sk_])
                yield
                A("act", C("activation", out=o1_[:], in_=a0[:, 0:128], func=AF.Copy, scale=s_[:, 0:1]), r=[k0, sk_], w=[("o1", i % 2)])
                yield
                A("dve", C("scalar_tensor_tensor", out=o2_[:], in0=a1[:, 0:128], scalar=s_[:, 1:2], in1=o1_[:], op0=ALU.mult, op1=ALU.add),
                  r=[k1, sk_, ("o1", i % 2)], w=[("o2", i % 2)])
                yield
                A("act", C("activation", out=on_[:], in_=o2_[:], func=AF.Square, accum_out=s_[:, 2:3]), r=[("o2", i % 2)], w=[sk_, ("onb", i % 2)])
                yield
                rsqrt(s_[:, 3:4], s_[:, 2:3], 128 * EPS, [sk_])
                yield
                A("dve", C("scalar_tensor_tensor", out=on_[:], in0=o2_[:], scalar=s_[:, 3:4], in1=sgb[:], op0=ALU.mult, op1=ALU.mult),
                  r=[("o2", i % 2), sk_, "sgb"], w=[("onb", i % 2)])
                yield
                A("pe", C("transpose", psT[:, 512:640], on_[:], ident[:]), r=[("onb", i % 2), "ident"], w=["psT"])
                A("act", C("copy", out=oT[ob_][:, (4 * j + i) * 128:(4 * j + i + 1) * 128], in_=psT[:, 512:640]), r=["psT"], w=[("oT", ob_, j)])
                yield

        def attn_gen(hd):
            b = hd % 2
            ei = [0]
            si = [0]

            def acc(c, i):
                if i < 3:
                    return psA[c][:, i * 129:(i + 1) * 129], ("psA", c)
                return psA[2][:, c * 129:(c + 1) * 129], ("psA", 2)

            pend = None
            for j in range(4):
                steps = [(kb, c) for kb in range(4 * j + 4) for c in range(2)]
                info = {}

                def issue_S(idx):
                    kb, c = steps[idx]
                    nq0 = max(kb, 4 * j)
                    N = (4 * j + 4 - nq0) * 128
                    sbk, sk = SB3[si[0] % 3]
                    si[0] += 1
                    diag = kb >= 4 * j
                    A("pe", C("matmul", sbk[:, 0:N], kT[b][64 * c:64 * c + 64, kb * 128:(kb + 1) * 128],
                              qT[b][64 * c:64 * c + 64, nq0 * 128:nq0 * 128 + N], start=True, stop=not diag, skip_group_check=True),
                      r=[("kT", b, kb // 4), ("qT", b, j)], w=[sk])
                    if diag:
                        A("pe", C("matmul", sbk[:, 0:128], negi[:], masks[:, 0, 128:256], start=False, stop=True, skip_group_check=True),
                          r=["negi", "masks"], w=[sk])
                    info[idx] = (sbk, sk, nq0, N)

                issue_S(0)
                issue_S(1)
                for idx, (kb, c) in enumerate(steps):
                    sbk, sk, nq0, N = info[idx]
                    E = Et[ei[0] % 4]
                    ek = ("E", ei[0] % 4)
                    ei[0] += 1
                    A("act", C("activation", out=E[:, 0:N], in_=sbk[:, 0:N], func=AF.Exp, scale=0.125), r=[sk], w=[ek])
                    if idx + 2 < len(steps):
                        issue_S(idx + 2)
                    for qb in range(nq0, 4 * j + 4):
                        ap_, ak = acc(c, qb - 4 * j)
                        i_ = qb - 4 * j
                        A("pe", C("matmul", ap_, E[:, (qb - nq0) * 128:(qb - nq0 + 1) * 128], vh[b][:, kb, 0:129],
                                  start=(kb == 0 and (i_ == 0 or (i_ == 3 and c == 0))), stop=(kb == qb), skip_group_check=True),
                          r=[ek, ("vh", b, kb)], w=[ak])
                    if pend is not None:
                        try:
                            next(pend)
                            next(pend)
                        except StopIteration:
                            pend = None
                    yield
                if pend is not None:
                    for _ in pend:
                        yield
                    pend = None
                buf = j % 2
                A("act", C("copy", out=accc[buf][0][:], in_=psA[0][:, 0:387]), r=[("psA", 0)], w=[("accc", buf, 0)])
                A("dve", C("tensor_copy", out=accc[buf][1][:], in_=psA[1][:, 0:387]), r=[("psA", 1)], w=[("accc", buf, 1)])
                A("act", C("copy", out=acc3[buf][:], in_=psA[2][:, 0:258]), r=[("psA", 2)], w=[("acc3", buf)])
                pend = fin_gen(hd, j, buf)
                yield
            for _ in pend:
                yield

        def out_pair_gen(pr):
            hs = (2 * pr, 2 * pr + 1)
            for q, h_ in enumerate(hs):
                dma_pool(wos[q][:], a_w_o[l][h_ * 128:(h_ + 1) * 128, :], w=[("wos", q)])
            yield
            i = 0
            for t in range(NT):
                for h in range(2):
                    pb, pk = psP[0], ("psP", 0)
                    for q, h_ in enumerate(hs):
                        A("pe", C("matmul", pb[:], oT[h_ % 4][:, t * 128:(t + 1) * 128], wos[q][:, h * 512:(h + 1) * 512],
                                  start=(q == 0), stop=(q == 1)), r=[("oT", h_ % 4, t // 4), ("wos", q)], w=[pk])
                    A("dve", C("tensor_tensor", out=xres[:, t, h * 512:(h + 1) * 512], in0=pb[:], in1=xres[:, t, h * 512:(h + 1) * 512], op=ALU.add),
                      r=[pk, ("x", t)], w=[("x", t)])
                    i += 1
                    yield

        drain(proj_gen(0))
        pending_out = None
        for hd in range(8):
            gens = [attn_gen(hd)]
            wts = [125.0]
            if hd + 1 < 8:
                gens.append(proj_gen(hd + 1))
                wts.append(82.0)
            if pending_out is not None:
                gens.append(pending_out)
                wts.append(33.0)
                pending_out = None
            mergeN(gens, wts)
            if hd % 2 == 1:
                pending_out = out_pair_gen(hd // 2)
        drain(pending_out)

    def layer_b2(bl):
        cv = Carver()
        wsl = cv.take([128, NCH, 384], BF16)
        wos = cv.take([128, D], BF16)
        qT3 = [cv.take([128, 3, S], BF16) for _ in range(2)]
        kTg = [cv.take([128, S], BF16) for _ in range(2)]
        vaug = [cv.take([128, NT, 192], BF16) for _ in range(2)]
        accS = [cv.take([128, S], F32) for _ in range(2)]
        oT = cv.take([128, S], BF16)
        rcp = cv.take([128, 512], F32)
        qkb = [cv.take([128, 384], BF16) for _ in range(3)]
        yps = [cv.take([128, 384], F32)] * 2
        Et = [cv.take([128, 512], BF16) for _ in range(3)]
        tmp = dict(sq=cv.take([128, 384], F32), ya=cv.take([128, 384], F32), yb=cv.take([128, 384], F32),
                   rs=cv.take([128, 16], F32), key="tmpB")
        rmsnorm_to_hT(b_norm[bl])
        dma_sp(gst[:, 0:192], b_q_gain[bl].rearrange("g d -> (g d)").partition_broadcast(128), r=["qkg"], w=["gst"])
        A("dve", C("tensor_scalar", out=qkg[:, 0:384].rearrange("p (g a d) -> p g a d", a=2, d=64),
                   in0=gst[:, 0:192].rearrange("p (g d) -> p g d", d=64).unsqueeze(2).to_broadcast([128, 3, 2, 64]),
                   scalar1=8.0, scalar2=None, op0=ALU.mult), r=["gst"], w=["qkg"])
        for bb in range(2):
            A("dve", C("memset", vaug[bb][:, :, 64:128], 1.0), w=[("vaug", bb, 0), ("vaug", bb, 1), ("vaugo", bb)])
        wq = b_w_q[bl].rearrange("(c p) n -> p c n", p=128)
        PP, PPK = psP[0], ("psP", 0)
        cnt_ = dict(ei=0, si=0, oi=0, li=0)

        def proj_gen(j):
            b = j % 2
            kws = [("wsl", g) for g in range(3)]
            for g in range(3):
                dma_pool(wsl[:, :, g * 128:(g + 1) * 128], wq[:, :, g * D + j * 128: g * D + (j + 1) * 128], w=[kws[g]])
            for t in range(NT + 2):
                if t >= 2:
                    tp = t - 2
                    qb_ = qkb[tp % 3]
                    for g in range(3):
                        A("pe", C("transpose", psT[:, g * 128:(g + 1) * 128], qb_[:, g * 128:(g + 1) * 128], ident[:]),
                          r=[("qkb", tp % 3), "ident"], w=["psT"])
                    src = psT[:, 0:384].rearrange("p (a k) -> p a k", k=128)
                    A("act", C("copy", out=qT3[b][:, :, tp * 128:(tp + 1) * 128], in_=src), r=["psT"], w=[("qT3", b, tp)])
                    yield
                if t < NT:
                    for c in range(NCH):
                        A("pe", C("matmul", PP[:, 0:384], hT[:, c, t * 128:(t + 1) * 128], wsl[:, c, :], start=(c == 0), stop=(c == NCH - 1)),
                          r=[("hT", t)] + kws, w=[PPK])
                    y_ = yps[t % 2]
                    yk = ("yps", 0)
                    A("act", C("copy", out=y_[:], in_=PP[:, 0:384]), r=[PPK], w=[yk])
                    for _ in qk_norm_rope_g(y_[:, 0:384], 6, t, qkb[t % 3][:], [yk], ("qkb", t % 3), tmp):
                        yield

        def attn_gen(j):
            b = j % 2
            allq = [("qT3", b, t) for t in range(NT)]
            dma_pool(wos[:], b_w_o[bl][j * 128:(j + 1) * 128, :], w=["wos"])
            for g in range(3):
                r_ = GROUPS[g][1]
                nb = 16 // r_
                bb = cnt_["li"] % 2
                cnt_["li"] += 1
                half, hh = j // 4, 2 * (j % 4)
                dma_sp(kTg[bb][:], kscr[g, j], r=[("kscr", g, j)], w=[("kTg", bb)])
                dma_sp(vaug[bb][:, :, 0:64], vscr[g, half][:, :, hh * 64:(hh + 1) * 64], r=[("vscr", g, half)], w=[("vaug", bb, 0)])
                dma_sp(vaug[bb][:, :, 128:192], vscr[g, half][:, :, (hh + 1) * 64:(hh + 2) * 64], r=[("vscr", g, half)], w=[("vaug", bb, 1)])
                vkeys = [("vaug", bb, 0), ("vaug", bb, 1), ("vaugo", bb)]
                batches = [(X, quad, h2) for X in range(2) for quad in range(4) for h2 in range(2)]
                info = {}

                def issue_S(idx):
                    X, quad, h2 = batches[idx]
                    rows = slice(64 * X, 64 * X + 64)
                    blks = (4 * quad + 2 * h2, 4 * quad + 2 * h2 + 1)
                    sbk, sk = SB3[cnt_["si"] % 3]
                    cnt_["si"] += 1
                    for ub, blk in enumerate(blks):
                        c_, n_ = blk // nb, blk % nb
                        qap = strided_tok(qT3[b][rows, g, :], r_, c_, n_)
                        for u2, kn in enumerate((n_ - 1 if n_ > 0 else n_, n_)):
                            u = 2 * ub + u2
                            A("pe", C("matmul", sbk[:, u * 128:(u + 1) * 128], strided_tok(kTg[bb][rows, :], r_, c_, kn), qap,
                                      start=(u == 0), stop=False, skip_group_check=True), r=[("kTg", bb)] + allq, w=[sk])
                    n0, n1 = blks[0] % nb, blks[1] % nb
                    mi = 0 if (n0 > 0 and n1 > 0) else (1 if n1 > 0 else 2)
                    A("pe", C("matmul", sbk[:], negi[:], masks[:, mi, :], start=False, stop=True, skip_group_check=True),
                      r=["negi", "masks"], w=[sk])
                    info[idx] = (sbk, sk, blks)

                issue_S(0)
                issue_S(1)
                ob = ok = None
                for idx, (X, quad, h2) in enumerate(batches):
                    sbk, sk, blks = info[idx]
                    vc = slice(64 * X, 64 * X + 128)
                    if h2 == 0:
                        ob = psA[cnt_["oi"] % 3]
                        ok = ("psA", cnt_["oi"] % 3)
                        cnt_["oi"] += 1
                    E = Et[cnt_["ei"] % 3]
                    ek = ("E", cnt_["ei"] % 3)
                    cnt_["ei"] += 1
                    A("act", C("activation", out=E[:], in_=sbk[:], func=AF.Exp, scale=0.125), r=[sk], w=[ek])
                    if idx + 2 < len(batches):
                        issue_S(idx + 2)
                    for ub, blk in enumerate(blks):
                        n_ = blk % nb
                        slot = blk - 4 * quad
                        oap = ob[:, slot * 128:(slot + 1) * 128]
                        if n_ > 0:
                            A("pe", C("matmul", oap, vaug[bb][:, blk - 1, vc], E[:, (2 * ub) * 128:(2 * ub + 1) * 128],
                                      start=True, stop=False), r=[ek] + vkeys, w=[ok])
                        A("pe", C("matmul", oap, vaug[bb][:, blk, vc], E[:, (2 * ub + 1) * 128:(2 * ub + 2) * 128],
                                  start=(n_ == 0), stop=True), r=[ek] + vkeys, w=[ok])
                    if h2 == 1:
                        accv = accS[X].rearrange("p (n b r) -> p r n b", b=128, r=r_)
                        if g == 0:
                            dst = accv[:, 0, 4 * quad:4 * quad + 4, :]
                        elif g == 1:
                            dst = accv[:, quad, 0:4, :]
                        else:
                            dst = accv[:, 4 * quad:4 * quad + 4, 0, :]
                        srcv = ob[:].rearrange("p (a k) -> p a k", k=128)
                        if g == 0:
                            A("act", C("copy", out=dst, in_=srcv), r=[ok], w=[("acc", X)])
                        else:
                            A("dve", C("tensor_tensor", out=dst, in0=srcv, in1=dst, op=ALU.add), r=[ok, ("acc", X)], w=[("acc", X)])
                    yield
            for X in range(2):
                rows = slice(64 * X, 64 * X + 64)
                for ck in range(4):
                    ob = psA[cnt_["oi"] % 3]
                    ok = ("psA", cnt_["oi"] % 3)
                    cnt_["oi"] += 1
                    A("pe", C("matmul", ob[:], swf[:], accS[X][:, ck * 512:(ck + 1) * 512], start=True, stop=True),
                      r=[("acc", X), "swf"], w=[ok])
                    A("dve", C("reciprocal", out=rcp[rows, :], in_=ob[rows, :]), r=[ok], w=[("rcp", X)])
                    A("dve", C("tensor_tensor", out=oT[rows, ck * 512:(ck + 1) * 512], in0=accS[X][rows, ck * 512:(ck + 1) * 512],
                               in1=rcp[rows, :], op=ALU.mult), r=[("rcp", X), ("acc", X)], w=[("oTb", X, ck)])
                    yield
            for _ in out_proj_gen(oT, lambda t: [("oTb", 0, t // 4), ("oTb", 1, t // 4)], wos, "wos"):
                yield

        drain(proj_gen(0))
        for j in range(8):
            if j + 1 < 8:
                merge(attn_gen(j), 72, proj_gen(j + 1), 82)
            else:
                drain(attn_gen(j))

    def mlp(l):
        cv = Carver()
        uT = cv.take([128, 32, 512], BF16)
        wup = [cv.take([128, NCH, 512], BF16) for _ in range(3)]
        wdn = [cv.take([128, 4, 512], BF16) for _ in range(6)]
        rl = [cv.take([128, 512], BF16) for _ in range(2)]
        rmsnorm_to_hT(m_norm[l])
        wu = m_w_up[l].rearrange("(c p) n -> p c n", p=128)
        wd = m_w_down[l].rearrange("(f p) n -> p f n", p=128)
        iu = 0
        idn = 0
        for tc4 in range(4):
            toks = slice(tc4 * 512, (tc4 + 1) * 512)
            for fg in range(8):
                wb = wup[iu % 3]
                wk = ("wup", iu % 3)
                iu += 1
                dma_pool(wb[:], wu[:, :, fg * 512:(fg + 1) * 512], w=[wk])
                for f4 in range(4):
                    f = fg * 4 + f4
                    pb = psP[f % 2]
                    pk = ("psP", f % 2)
                    for c in range(NCH):
                        A("pe", C("matmul", pb[:], wb[:, c, f4 * 128:(f4 + 1) * 128], hT[:, c, toks],
                                                                             start=(c == 0), stop=(c == NCH - 1)),
                          r=[wk] + [("hT", 4 * tc4 + q) for q in range(4)], w=[pk])
                    r_ = rl[f % 2]
                    A("act", C("activation", out=r_[:], in_=pb[:], func=AF.Relu), r=[pk], w=[("rl", f % 2)])
                    A("dve", C("tensor_tensor", out=uT[:, f, :], in0=r_[:], in1=r_[:], op=ALU.mult),
                      r=[("rl", f % 2)], w=[("uT", f)])
            for h in range(2):
                banks = [(psA[0], ("psA", 0)), (psA[1], ("psA", 1)), (psA[2], ("psA", 2)), (psS[0], ("psS", 0))]
                for fg in range(8):
                    wb = wdn[idn % 6]
                    wk = ("wdn", idn % 6)
                    idn += 1
                    dma_pool(wb[:], wd[:, fg * 4:(fg + 1) * 4, h * 512:(h + 1) * 512], w=[wk])
                    for q in range(4):
                        pb, pk = banks[q]
                        for f4 in range(4):
                            f = fg * 4 + f4
                            A("pe", C("matmul", pb[:], uT[:, f, q * 128:(q + 1) * 128], wb[:, f4, :],
                                                                                     start=(f == 0), stop=(f == 31)),
                              r=[("uT", f), wk], w=[pk])
                for q in range(4):
                    pb, pk = banks[q]
                    t = 4 * tc4 + q
                    A("dve", C("tensor_tensor", out=xres[:, t, h * 512:(h + 1) * 512], in0=pb[:],
                                                                        in1=xres[:, t, h * 512:(h + 1) * 512], op=ALU.add),
                      r=[pk, ("x", t)], w=[("x", t)])

    kv_done = False
    try:
      ck("load")
      for l in layers:
        sch.barrier()
        if l < 2:
            layer_a2(l)
        else:
            if not kv_done:
                kv_phase()
                kv_done = True
                sch.barrier()
            layer_b2(l - 2)
        ck("mixer")
        sch.barrier()
        mlp(l)
    except StopBuild:
        pass
    sch.barrier()
    yv = y_out.rearrange("(t p) d -> p t d", p=128)
    outs = []
    for g4 in range(4):
        outs.append(dma_sp(yv[:, 4 * g4:4 * g4 + 4, :], xres[:, 4 * g4:4 * g4 + 4, :], r=[("x", t) for t in range(4 * g4, 4 * g4 + 4)]))
    A("sp", None, after=outs, name="final")

    sch.finalize()
    with nc.Block() as block:
        @block.tensor
        def _(e):
            sch.emit("pe", e, sems, dsems)

        @block.scalar
        def _(e):
            sch.emit("act", e, sems, dsems)

        @block.vector
        def _(e):
            sch.emit("dve", e, sems, dsems)

        @block.gpsimd
        def _(e):
            sch.emit("pool", e, sems, dsems)

        @block.sync
        def _(e):
            sch.emit("sp", e, sems, dsems)
    es.close()
    return nc


_NC_CACHE = {}


def kernel(**inputs):
    names = ["a_norm", "a_w_qkv", "a_q_gain", "a_k_gain", "a_lam_q1", "a_lam_k1", "a_lam_q2", "a_lam_k2", "a_sub_gain",
             "a_w_o", "kv_norm", "kv_w", "kv_k_gain", "b_norm", "b_w_q", "b_q_gain", "b_w_o", "m_norm", "m_w_up", "m_w_down"]
    shared = {n: np.ascontiguousarray(np.asarray(inputs[n], dtype=np.float32)) for n in names}
    shared.update(_consts())
    x = np.asarray(inputs["x"], dtype=np.float32)
    nc = build()
    in_maps = []
    for i in range(8):
        m = dict(shared)
        m["x"] = np.ascontiguousarray(x[i])
        in_maps.append(m)
    res = run_bass_kernel_spmd(nc, in_maps, core_ids=list(range(8)))
    return np.stack([np.asarray(r["y"], dtype=np.float32) for r in res.results], axis=0)
```
